# Optimizing a Trainium2 kernel written in Bass

```python
import jax
import jax.numpy as jnp
from jax import lax
import numpy as np

D_MODEL = 1024
BATCH = 4
SEQ = 4096
DEPTH = 1

HGRN_HEADS = 8
HGRN_DK = D_MODEL // HGRN_HEADS
HGRN_DV = D_MODEL // HGRN_HEADS
HGRN_CHUNK = 64
MOBA_HEADS = 8
MOBA_HD = D_MODEL // MOBA_HEADS
MOBA_BLOCK = 256
MOBA_TOPK = 3
MOBA_QCHUNK = 16
ROPE_THETA = 10000.0
D_FF = ((8 * D_MODEL + 3 * 256 - 1) // (3 * 256)) * 256
N_PROJ = 2 * HGRN_HEADS * HGRN_DK + 2 * HGRN_HEADS * HGRN_DV + 3 * MOBA_HEADS * MOBA_HD + 2 * D_MODEL
DN_ALPHA = (2.0 * DEPTH) ** 0.25
DN_BETA = (8.0 * DEPTH) ** -0.25
LN_EPS = 1e-5
RMS_EPS = 1e-6

kernel_name = 'hybrid_hgrn2_moba_deepnorm'


def layer_norm(x, w, b):
    xf = x.astype(jnp.float32)
    mu = jnp.mean(xf, axis=-1, keepdims=True)
    var = jnp.mean(jnp.square(xf - mu), axis=-1, keepdims=True)
    y = (xf - mu) * lax.rsqrt(var + LN_EPS) * w.astype(jnp.float32) + b.astype(jnp.float32)
    return y.astype(x.dtype)


def rms_norm(x, w):
    xf = x.astype(jnp.float32)
    y = xf * lax.rsqrt(jnp.mean(jnp.square(xf), axis=-1, keepdims=True) + RMS_EPS)
    return y * w.astype(jnp.float32)


def apply_rope(t):
    S, hd = t.shape[1], t.shape[-1]
    half = hd // 2
    inv_freq = ROPE_THETA ** (-jnp.arange(half, dtype=jnp.float32) / half)
    ang = jnp.arange(S, dtype=jnp.float32)[:, None] * inv_freq[None, :]
    cos = jnp.cos(ang)[None, :, None, :]
    sin = jnp.sin(ang)[None, :, None, :]
    tf = t.astype(jnp.float32)
    t1, t2 = tf[..., :half], tf[..., half:]
    return jnp.concatenate([t1 * cos - t2 * sin, t2 * cos + t1 * sin], axis=-1).astype(t.dtype)


def hgrn2_mixer(q, f_logit, inp, lb):
    B, S, H, dk = q.shape
    dv = inp.shape[-1]
    C = HGRN_CHUNK
    nc = S // C
    lb = lb.reshape(H, dk)
    z = f_logit.astype(jnp.float32)
    log_f = jnp.logaddexp(jnp.log(lb), jnp.log1p(-lb) + jax.nn.log_sigmoid(z))
    k = (1.0 - lb) * jax.nn.sigmoid(-z)
    qf = jax.nn.silu(q.astype(jnp.float32))
    v = inp.astype(jnp.float32)

    def to_chunks(t):
        return t.reshape(B, nc, C, H, t.shape[-1]).transpose(1, 0, 3, 2, 4)

    causal = jnp.tril(jnp.ones((C, C), dtype=bool))[:, :, None]

    def step(state, xs):
        qc, kc, vc, lfc = xs
        G = jnp.cumsum(lfc, axis=2)
        diff = G[:, :, :, None, :] - G[:, :, None, :, :]
        decay = jnp.exp(jnp.where(causal, diff, -jnp.inf))
        scores = jnp.einsum('bhtsd,bhtd,bhsd->bhts', decay, qc, kc)
        o = jnp.einsum('bhts,bhsv->bhtv', scores, vc) + jnp.einsum('bhtd,bhdv->bhtv', qc * jnp.exp(G), state)
        g_last = G[:, :, -1, :]
        new_state = jnp.exp(g_last)[..., None] * state + jnp.einsum(
            'bhsd,bhsv->bhdv', kc * jnp.exp(g_last[:, :, None, :] - G), vc)
        return new_state, o

    s0 = jnp.zeros((B, H, dk, dv), jnp.float32)
    _, o = lax.scan(step, s0, (to_chunks(qf), to_chunks(k), to_chunks(v), to_chunks(log_f)))
    return o.transpose(1, 0, 3, 2, 4).reshape(B, S, H, dv)


def moba_attention(q, k, v):
    B, S, H, hd = q.shape
    BLK = MOBA_BLOCK
    QC = MOBA_QCHUNK
    nb = -(-S // BLK)
    Sp = nb * BLK
    nq = S // QC
    topk = min(MOBA_TOPK, nb)
    scale = hd ** -0.5
    qh = q.transpose(0, 2, 1, 3)
    pad = ((0, 0), (0, 0), (0, Sp - S), (0, 0))
    kb = jnp.pad(k.transpose(0, 2, 1, 3), pad).reshape(B, H, nb, BLK, hd)
    vb = jnp.pad(v.transpose(0, 2, 1, 3), pad).reshape(B, H, nb, BLK, hd)
    k_mean = jnp.mean(kb.astype(jnp.float32), axis=3)
    gate = jnp.einsum('bhsd,bhnd->bhsn', qh.astype(jnp.float32), k_mean)
    own = jnp.arange(S) // BLK
    past = jnp.arange(nb)[None, :] < own[:, None]
    gate = jnp.where(past, gate, -jnp.inf)
    _, sel = lax.top_k(gate, topk)

    q_ch = qh.reshape(B, H, nq, QC, hd).transpose(2, 0, 1, 3, 4)
    sel_ch = sel.reshape(B, H, nq, QC, topk).transpose(2, 0, 1, 3, 4)
    gather = jax.vmap(jax.vmap(lambda blocks, ix: blocks[ix]))

    def attend(args):
        ci, qc, sc = args
        start = ci * QC
        qpos = start + jnp.arange(QC)
        ob = start // BLK
        kpos = ob * BLK + jnp.arange(BLK)
        k_own = lax.dynamic_index_in_dim(kb, ob, axis=2, keepdims=False)
        v_own = lax.dynamic_index_in_dim(vb, ob, axis=2, keepdims=False)
        k_sel = gather(kb, sc)
        v_sel = gather(vb, sc)
        s_sel = jnp.einsum('bhqd,bhqnkd->bhqnk', qc, k_sel).astype(jnp.float32) * scale
        valid = sc < (qpos // BLK)[:, None]
        s_sel = jnp.where(valid[..., None], s_sel, -jnp.inf).reshape(B, H, QC, topk * BLK)
        s_own = jnp.einsum('bhqd,bhkd->bhqk', qc, k_own).astype(jnp.float32) * scale
        s_own = jnp.where(kpos[None, :] <= qpos[:, None], s_own, -jnp.inf)
        p = jax.nn.softmax(jnp.concatenate([s_sel, s_own], axis=-1), axis=-1).astype(qc.dtype)
        p_sel = p[..., :topk * BLK].reshape(B, H, QC, topk, BLK)
        p_own = p[..., topk * BLK:]
        return (jnp.einsum('bhqnk,bhqnkd->bhqd', p_sel, v_sel)
                + jnp.einsum('bhqk,bhkd->bhqd', p_own, v_own))

    o = lax.map(attend, (jnp.arange(nq), q_ch, sel_ch))
    return o.transpose(1, 0, 3, 2, 4).reshape(B, S, H, hd)


def hybrid_layer(x, w_in, lb, hgrn_norm_w, w_branch_a, w_branch_b, b_gate, w_out,
                 ln1_w, ln1_b, w_ffn_in, w_ffn_down, ln2_w, ln2_b):
    B, S, D = x.shape
    proj = jnp.einsum('bsd,dn->bsn', x, w_in)
    sizes = (HGRN_HEADS * HGRN_DK, HGRN_HEADS * HGRN_DK, HGRN_HEADS * HGRN_DV, HGRN_HEADS * HGRN_DV,
             MOBA_HEADS * MOBA_HD, MOBA_HEADS * MOBA_HD, MOBA_HEADS * MOBA_HD, 2 * D_MODEL)
    offsets = np.cumsum(sizes)[:-1].tolist()
    hq, hf, hi, hg, mq, mk, mv, gate_logits = jnp.split(proj, offsets, axis=-1)

    o_a = hgrn2_mixer(hq.reshape(B, S, HGRN_HEADS, HGRN_DK), hf.reshape(B, S, HGRN_HEADS, HGRN_DK),
                      hi.reshape(B, S, HGRN_HEADS, HGRN_DV), lb)
    y_a = rms_norm(o_a, hgrn_norm_w.reshape(HGRN_HEADS, HGRN_DV)) * jax.nn.silu(
        hg.reshape(B, S, HGRN_HEADS, HGRN_DV).astype(jnp.float32))
    y_a = y_a.reshape(B, S, HGRN_HEADS * HGRN_DV).astype(x.dtype)

    q_b = apply_rope(mq.reshape(B, S, MOBA_HEADS, MOBA_HD))
    k_b = apply_rope(mk.reshape(B, S, MOBA_HEADS, MOBA_HD))
    y_b = moba_attention(q_b, k_b, mv.reshape(B, S, MOBA_HEADS, MOBA_HD)).reshape(B, S, MOBA_HEADS * MOBA_HD)

    z_a = jnp.einsum('bsc,cd->bsd', y_a, w_branch_a)
    z_b = jnp.einsum('bsc,cd->bsd', y_b, w_branch_b)
    g_a, g_b = jnp.split(jax.nn.sigmoid(gate_logits + b_gate), 2, axis=-1)
    mixed = jnp.einsum('bsc,cd->bsd', g_a * z_a + g_b * z_b, w_out)
    x = layer_norm(DN_ALPHA * x + mixed, ln1_w, ln1_b)

    h = jnp.einsum('bsd,df->bsf', x, w_ffn_in)
    h_gate, h_up = jnp.split(h, 2, axis=-1)
    y = jnp.einsum('bsf,fd->bsd', jax.nn.silu(h_gate) * h_up, w_ffn_down)
    return layer_norm(DN_ALPHA * x + y, ln2_w, ln2_b)


def setup_inputs(seed: int = 0) -> dict:
    key = jax.random.key(seed)
    ks = jax.random.split(key, 16)

    def nrm(k, shape, scale):
        return jax.random.normal(k, shape, jnp.float32) * scale

    d_a = HGRN_HEADS * HGRN_DV
    d_b = MOBA_HEADS * MOBA_HD
    return {
        'x': nrm(ks[0], (BATCH, SEQ, D_MODEL), 1.0),
        'w_in': nrm(ks[1], (DEPTH, D_MODEL, N_PROJ), D_MODEL ** -0.5),
        'lb_logits': nrm(ks[2], (DEPTH + 1, HGRN_HEADS * HGRN_DK), 0.1),
        'hgrn_norm_w': 1.0 + nrm(ks[3], (DEPTH, d_a), 0.02),
        'w_branch_a': nrm(ks[4], (DEPTH, d_a, D_MODEL), d_a ** -0.5),
        'w_branch_b': nrm(ks[5], (DEPTH, d_b, D_MODEL), d_b ** -0.5),
        'b_gate': nrm(ks[6], (DEPTH, 2 * D_MODEL), 0.1),
        'w_out': nrm(ks[7], (DEPTH, D_MODEL, D_MODEL), D_MODEL ** -0.5 * DN_BETA),
        'ln1_w': 1.0 + nrm(ks[8], (DEPTH, D_MODEL), 0.02),
        'ln1_b': nrm(ks[9], (DEPTH, D_MODEL), 0.02),
        'w_ffn_in': nrm(ks[10], (DEPTH, D_MODEL, 2 * D_FF), D_MODEL ** -0.5),
        'w_ffn_down': nrm(ks[11], (DEPTH, D_FF, D_MODEL), D_FF ** -0.5 * DN_BETA),
        'ln2_w': 1.0 + nrm(ks[12], (DEPTH, D_MODEL), 0.02),
        'ln2_b': nrm(ks[13], (DEPTH, D_MODEL), 0.02),
    }


def reference(x, w_in, lb_logits, hgrn_norm_w, w_branch_a, w_branch_b, b_gate, w_out,
              ln1_w, ln1_b, w_ffn_in, w_ffn_down, ln2_w, ln2_b):
    lower_bounds = jnp.cumsum(jax.nn.softmax(lb_logits.astype(jnp.float32), axis=0), axis=0)
    h = x
    for l in range(DEPTH):
        h = hybrid_layer(h, w_in[l], lower_bounds[l], hgrn_norm_w[l], w_branch_a[l], w_branch_b[l],
                         b_gate[l], w_out[l], ln1_w[l], ln1_b[l], w_ffn_in[l], w_ffn_down[l],
                         ln2_w[l], ln2_b[l])
    return h
```

```python
import numpy as np
from contextlib import ExitStack
import concourse.bass as bass
import concourse.mybir as mybir
from concourse.bass_utils import run_bass_kernel_spmd

F32 = mybir.dt.float32
BF16 = mybir.dt.bfloat16
ALU = mybir.AluOpType
AF = mybir.ActivationFunctionType
AX = mybir.AxisListType

D = 1024
T = 2048
NH = 8
DFF = 2816
NFC = DFF // 128
ALPHA = 2.0 ** 0.25
LN_EPS = 1e-5
RMS_EPS = 1e-6
SCALE = 128.0 ** -0.5
NEG = -1.0e30


class Eng:
    def __init__(self, name, e, sem):
        self.name = name
        self.e = e
        self.sem = sem
        self.cnt = 0
        self.seen = {}


class DmaSem:
    def __init__(self, sem):
        self.sem = sem
        self.cnt = 0
        self.name = "dma"


class Buf:
    __slots__ = ("t", "w", "r", "dsem", "name", "excl")

    def __init__(self, t, name=""):
        self.excl = False
        self.t = t
        self.w = None
        self.r = {}
        self.dsem = None
        self.name = name

    def __getitem__(self, k):
        return self.t[k]


class MK:
    def __init__(self, nc, stack):
        self.nc = nc
        self.stack = stack
        self.n_sem = 0
        self.dsems = []
        self.pe = self._eng("pe", nc.tensor)
        self.act = self._eng("act", nc.scalar)
        self.dve = self._eng("dve", nc.vector)
        self.pool = self._eng("pool", nc.gpsimd)
        self.sp = self._eng("sp", nc.sync)
        self.engs = [self.pe, self.act, self.dve, self.pool, self.sp]
        self.n_inst = 0

    def _sem(self, name):
        self.n_sem += 1
        return self.stack.enter_context(self.nc.semaphore(f"{name}_{self.n_sem}"))

    def _eng(self, name, e):
        return Eng(name, e, self._sem("s_" + name))

    def new_dsem(self, name="d"):
        d = DmaSem(self._sem(name))
        self.dsems.append(d)
        return d

    def buf(self, ap, name="", dsem=None):
        b = Buf(ap, name)
        b.dsem = dsem
        return b

    def _need(self, eng, need, src, v):
        if src is eng and eng is self.pe:
            return
        if eng.seen.get(src, 0) >= v:
            return
        if need.get(src, 0) < v:
            need[src] = v

    def _waits(self, eng, reads, writes):
        need = {}
        for b in reads:
            if b.w is not None:
                self._need(eng, need, *b.w)
            if b.excl:
                for src, v in b.r.items():
                    if src is not eng:
                        self._need(eng, need, src, v)
        for b in writes:
            if b.w is not None:
                self._need(eng, need, *b.w)
            for src, v in b.r.items():
                self._need(eng, need, src, v)
        for src, v in need.items():
            eng.e.wait_ge(src.sem, v)
            eng.seen[src] = v

    def _mark(self, d, reads, writes):
        src, v = d
        for b in reads:
            if b.r.get(src, 0) < v:
                b.r[src] = v
        for b in writes:
            b.w = d
            b.r = {}

    def op(self, eng, fn, reads=(), writes=(), inc=True):
        self._waits(eng, reads, writes)
        ins = fn(eng.e)
        self.n_inst += 1
        if inc:
            ins.then_inc(eng.sem, 1)
            eng.cnt += 1
            d = (eng, eng.cnt)
        else:
            d = (eng, eng.cnt + 1)
        self._mark(d, reads, writes)
        return d

    def dma(self, q, out_ap, in_ap, reads=(), writes=(), **kw):
        self._waits(q, reads, writes)
        tgt = None
        for b in list(writes) + list(reads):
            if b.dsem is not None:
                tgt = b.dsem
                break
        assert tgt is not None
        ins = q.e.dma_start(out=out_ap, in_=in_ap, **kw)
        ins.then_inc(tgt.sem, 16)
        tgt.cnt += 16
        d = (tgt, tgt.cnt)
        self.n_inst += 1
        self._mark(d, reads, writes)
        return d

    def barrier(self):
        srcs = [e for e in self.engs if e.cnt > 0] + [d for d in self.dsems if d.cnt > 0]
        for e in self.engs:
            for s in srcs:
                if e.seen.get(s, 0) < s.cnt:
                    e.e.wait_ge(s.sem, s.cnt)
                    e.seen[s] = s.cnt


class Arena:
    def __init__(self, m, nbytes):
        self.m = m
        self.nbytes = nbytes
        self.t = m.stack.enter_context(m.nc.sbuf_tensor("arena", [128, nbytes // 2], BF16))
        self.top = 0

    def at(self, off, shape, dtype, name="", dsem=None):
        n = 1
        for s in shape[1:]:
            n *= s
        esz = 2 if dtype == BF16 else 4
        assert off % 4 == 0 and off + n * esz <= self.nbytes, (name, off, n * esz, self.nbytes)
        ap = self.t[:, off // 2: off // 2 + n * esz // 2]
        if dtype != BF16:
            ap = ap.bitcast(dtype)
        if len(shape) == 3:
            ap = ap.rearrange("p (a b) -> p a b", b=shape[2])
        elif len(shape) == 4:
            ap = ap.rearrange("p (a b c) -> p a b c", b=shape[2], c=shape[3])
        return self.m.buf(ap, name, dsem)


class Bump:
    def __init__(self, arena, start):
        self.a = arena
        self.off = start

    def get(self, shape, dtype, name="", dsem=None):
        n = 1
        for s in shape[1:]:
            n *= s
        esz = 2 if dtype == BF16 else 4
        b = self.a.at(self.off, shape, dtype, name, dsem)
        self.off += (n * esz + 31) // 32 * 32
        return b


def build_program(stop_after=None, debug=False, nhB=NH, tgsB=tuple(range(8))):
    nc = bass.Bass("TRN2", target_bir_lowering=False)

    def din(name, shape):
        return nc.dram_tensor(name, list(shape), F32, kind="ExternalInput").ap()

    xT_own_d = din("xT_own", [128, 8, T])
    xT_pre_d = din("xT_pre", [128, 8, T])
    x_tok_d = din("x_tok", [T, D])
    w_in_d = din("w_in_r", [128, 8, 9216])
    w_a_d = din("w_a_r", [128, 8, D])
    w_b_d = din("w_b_r", [128, 8, D])
    w_o_d = din("w_o_r", [128, 8, D])
    w_fi_d = din("w_fi_r", [128, 8, 2 * DFF])
    w_fd_d = din("w_fd_r", [128, NFC, D])
    lbl_d = din("lbl", [128, 2, 8])
    nw_d = din("nw", [128, 8])
    bg_d = din("bg", [128, 16])
    lnp_d = din("lnp", [128, 4, D])
    cs_own_d = din("cs_own", [128, 2, T])
    cs_pre_d = din("cs_pre", [128, 2, T])
    tri_d = din("tri", [128, 128])
    ident_d = din("ident", [128, 128])
    scanm_d = din("scanm", [128, 512])
    gbias_d = din("gbias", [128, 16, 16])
    out_d = nc.dram_tensor("out", [T, D], F32, kind="ExternalOutput").ap()
    x1_d = nc.dram_tensor("x1_scratch", [T, D], F32, kind="Internal").ap()
    dbg = {}
    if debug:
        dbg["ya"] = nc.dram_tensor("dbg_ya", [128, 8, T], BF16, kind="ExternalOutput").ap()
        dbg["yb"] = nc.dram_tensor("dbg_yb", [128, 8, T], BF16, kind="ExternalOutput").ap()
        dbg["x1"] = nc.dram_tensor("dbg_x1", [T, D], F32, kind="ExternalOutput").ap()

    with ExitStack() as st:
        m = MK(nc, st)
        pe, act, dve, pool, sp = m.pe, m.act, m.dve, m.pool, m.sp
        PSt = st.enter_context(nc.psum_tensor("ps", [128, 8, 512], F32))
        AR = Arena(m, 207 * 1024)

        def new_banks():
            bs = [m.buf(PSt[:, i, :], f"ps{i}") for i in range(8)]
            for b_ in bs:
                b_.excl = True
            return bs

        SLOT = 8 * T * 2
        OFF_CONST = 4 * SLOT
        cb = Bump(AR, OFF_CONST)
        tri_f = cb.get([128, 128], F32, "tri_f", m.new_dsem())
        tri_b = cb.get([128, 128], BF16, "tri_b")
        ident_b = cb.get([128, 128], BF16, "ident_b", m.new_dsem())
        ones_b = cb.get([128, 128], BF16, "ones_b")
        scanm = cb.get([128, 512], F32, "scanm", m.new_dsem())
        lbl = cb.get([128, 2, 8], F32, "lbl", m.new_dsem())
        lbv = cb.get([128, 8], F32, "lbv")
        omlb = cb.get([128, 8], F32, "omlb")
        ln1mlb = cb.get([128, 8], F32, "ln1mlb")
        lbt = cb.get([128, 8], F32, "lbt")
        nw = cb.get([128, 8], F32, "nw", m.new_dsem())
        bgn = cb.get([128, 16], F32, "bgn", m.new_dsem())
        OFF_PHASE = cb.off

        xT_own = AR.at(0 * SLOT, [128, 8, T], BF16, "xT_own", m.new_dsem())
        xT_pre = AR.at(1 * SLOT, [128, 8, T], BF16, "xT_pre", m.new_dsem())
        ybT = AR.at(2 * SLOT, [128, 8, T], BF16, "ybT", m.new_dsem())
        yaT = AR.at(3 * SLOT, [128, 8, T], BF16, "yaT", m.new_dsem())

        m.dma(sp, tri_f[:], tri_d, writes=[tri_f])
        m.dma(pool, ident_b[:], ident_d, writes=[ident_b])
        m.dma(sp, scanm[:], scanm_d, writes=[scanm])
        m.dma(sp, lbl[:], lbl_d, writes=[lbl])
        m.dma(sp, nw[:], nw_d, writes=[nw])
        m.dma(sp, bgn[:], bg_d, writes=[bgn])
        for kc in range(8):
            m.dma(pool, xT_pre[:, kc, :], xT_pre_d[:, kc, :], writes=[xT_pre])
        for kc in range(8):
            m.dma(pool, xT_own[:, kc, :], xT_own_d[:, kc, :], writes=[xT_own])
        m.op(dve, lambda e: e.tensor_copy(tri_b[:], tri_f[:]), reads=[tri_f], writes=[tri_b])
        m.op(dve, lambda e: e.memset(ones_b[:], 1.0 / 128.0), writes=[ones_b])
        m.op(dve, lambda e: e.tensor_tensor(lbt[:], lbl[:, 1, :], lbl[:, 0, :], ALU.subtract), reads=[lbl], writes=[lbt])
        m.op(act, lambda e: e.activation(lbt[:], lbt[:], AF.Exp), reads=[lbt], writes=[lbt])
        m.op(dve, lambda e: e.tensor_scalar(lbv[:], lbt[:], 1.0, None, ALU.add), reads=[lbt], writes=[lbv])
        m.op(dve, lambda e: e.reciprocal(lbv[:], lbv[:]), reads=[lbv], writes=[lbv])
        m.op(dve, lambda e: e.tensor_tensor(omlb[:], lbt[:], lbv[:], ALU.mult), reads=[lbt, lbv], writes=[omlb])
        m.op(act, lambda e: e.activation(ln1mlb[:], omlb[:], AF.Ln), reads=[omlb], writes=[ln1mlb])
        m.op(dve, lambda e: e.tensor_scalar(bgn[:], bgn[:], -1.0, None, ALU.mult), reads=[bgn], writes=[bgn])

        if stop_after == "0":
            return _finish(nc, m, [], out_d)

        def proj_fm(bank, wt, j, xt, col0, ncol):
            for kc in range(8):
                m.op(pe, lambda e, kc=kc: e.matmul(bank[:, 0:ncol], lhsT=wt[:, j, kc, :], rhs=xt[:, kc, col0:col0 + ncol],
                                                   start=(kc == 0), stop=(kc == 7)),
                     reads=[wt, xt], writes=[bank], inc=(kc == 7))

        def proj_tm(bank, wt, j, xt, col0):
            for tt in range(4):
                for kc in range(8):
                    m.op(pe, lambda e, kc=kc, tt=tt: e.matmul(bank[:, tt * 128:(tt + 1) * 128],
                                                              lhsT=xt[:, kc, col0 + tt * 128: col0 + (tt + 1) * 128],
                                                              rhs=wt[:, j, kc, :], start=(kc == 0), stop=(kc == 7)),
                         reads=[wt, xt], writes=[bank], inc=(kc == 7 and tt == 3))

        def load_w(wslot, src_d, cols, ntile):
            for j in range(ntile):
                m.dma(pool, wslot[:, j, :, :], src_d[:, :, cols[j]:cols[j] + 128], writes=[wslot])

        pb = Bump(AR, OFF_PHASE)
        wsl = [pb.get([128, 4 * 8 * 128], BF16, f"wB{i}", m.new_dsem()) for i in range(2)]
        for b_ in wsl:
            b_.t = b_.t.rearrange("p (j k c) -> p j k c", j=4, k=8)
        NSET = 2
        W = [{} for _ in range(NSET)]
        for s_ in range(NSET):
            for nm in ["U", "L1", "L2", "G", "WT", "DM", "EDN", "UQ", "UH", "SG"]:
                W[s_][nm] = pb.get([128, 512], F32, f"{nm}{s_}")
            W[s_]["ED"] = W[s_]["U"]
            W[s_]["SQ"] = W[s_]["L2"]
            W[s_]["RS"] = W[s_]["L1"]
            W[s_]["Y"] = W[s_]["EDN"]
            for nm in ["KT", "QT", "SQO"]:
                W[s_][nm] = pb.get([128, 512], BF16, f"{nm}{s_}")
            W[s_]["VT"] = pb.get([128, 4, 128], BF16, f"VT{s_}")
            W[s_]["KTOK"] = pb.get([128, 4, 128], BF16, f"KTOK{s_}")
            W[s_]["EM"] = pb.get([128, 4], F32, f"EM{s_}")
            W[s_]["EL"] = pb.get([128, 4], F32, f"EL{s_}")
            W[s_]["ELM"] = pb.get([128, 4], F32, f"ELM{s_}")
        Sst = pb.get([128, 128], F32, "Sst")
        S2B = [pb.get([128, 128], BF16, f"S2B{i}") for i in range(2)]
        ASB = [pb.get([128, 128], BF16, f"ASB{i}") for i in range(2)]
        TMP = [pb.get([128, 128], F32, f"TMP{i}") for i in range(2)]
        assert pb.off <= AR.nbytes, pb.off
        ps = new_banks()

        def hcols(hd):
            return [0 * 1024 + hd * 128, 1 * 1024 + hd * 128, 2 * 1024 + hd * 128, 3 * 1024 + hd * 128]

        load_w(wsl[0], w_in_d, hcols(0), 4)
        cidx = 0
        for hd in range(nhB):
            wt = wsl[hd % 2]
            if hd + 1 < nhB:
                load_w(wsl[(hd + 1) % 2], w_in_d, hcols(hd + 1), 4)
            m.op(dve, lambda e: e.memset(Sst[:], 0.0), writes=[Sst])
            for tg in tgsB:
                own = tg >= 4
                xt = xT_own if own else xT_pre
                col0 = (tg % 4) * 512
                w_ = W[tg % NSET]
                proj_fm(ps[0], wt, 1, xt, col0, 512)
                proj_tm(ps[3], wt, 2, xt, col0)
                if own:
                    proj_fm(ps[1], wt, 0, xt, col0, 512)
                    proj_fm(ps[2], wt, 3, xt, col0, 512)
                U, L1, L2, G, WT, DM, EDN = w_["U"], w_["L1"], w_["L2"], w_["G"], w_["WT"], w_["DM"], w_["EDN"]
                KT, QT, VT, KTOK = w_["KT"], w_["QT"], w_["VT"], w_["KTOK"]
                EM, EL, ELM = w_["EM"], w_["EL"], w_["ELM"]
                m.op(act, lambda e: e.activation(U[:], ps[0][:], AF.Exp, scale=-1.0), reads=[ps[0]], writes=[U])
                m.op(act, lambda e: e.activation(L1[:], U[:], AF.Ln, bias=1.0), reads=[U], writes=[L1])
                m.op(act, lambda e: e.activation(L2[:], U[:], AF.Ln, scale=lbv[:, hd:hd + 1], bias=1.0), reads=[U, lbv], writes=[L2])
                m.op(dve, lambda e: e.tensor_tensor(L2[:], L2[:], L1[:], ALU.subtract), reads=[L2, L1], writes=[L2])
                m.op(dve, lambda e: e.tensor_tensor_scan(G[:], scanm[:], L2[:], 0.0, ALU.mult, ALU.add), reads=[scanm, L2], writes=[G])
                m.op(dve, lambda e: e.scalar_tensor_tensor(WT[:], ps[0][:], -1.0, L1[:], ALU.mult, ALU.subtract), reads=[ps[0], L1], writes=[WT])
                m.op(act, lambda e: e.activation(WT[:], WT[:], AF.Exp, bias=ln1mlb[:, hd:hd + 1]), reads=[WT, ln1mlb], writes=[WT])
                G3 = G[:].rearrange("p (c t) -> p c t", t=128)
                D3 = DM[:].rearrange("p (c t) -> p c t", t=128)
                m.op(dve, lambda e: e.tensor_tensor(D3, G3, G3[:, :, 63:64].to_broadcast([128, 4, 128]), ALU.subtract), reads=[G], writes=[DM])
                m.op(act, lambda e: e.activation(EDN[:], DM[:], AF.Exp, scale=-1.0), reads=[DM], writes=[EDN])
                m.op(dve, lambda e: e.tensor_tensor(KT[:], WT[:], EDN[:], ALU.mult), reads=[WT, EDN], writes=[KT])
                m.op(act, lambda e: e.activation(EM[:], G3[:, :, 63], AF.Exp), reads=[G], writes=[EM])
                m.op(act, lambda e: e.activation(EL[:], G3[:, :, 127], AF.Exp), reads=[G], writes=[EL])
                m.op(dve, lambda e: e.tensor_tensor(ELM[:], G3[:, :, 127], G3[:, :, 63], ALU.subtract), reads=[G], writes=[ELM])
                m.op(act, lambda e: e.activation(ELM[:], ELM[:], AF.Exp), reads=[ELM], writes=[ELM])
                m.op(act, lambda e: e.activation(VT[:].rearrange("p a b -> p (a b)"), ps[3][:], AF.Copy), reads=[ps[3]], writes=[VT])
                p4b = ps[4][:].bitcast(BF16)
                for c in range(4):
                    m.op(pe, lambda e, c=c: e.transpose(p4b[:, c * 128:(c + 1) * 128], KT[:, c * 128:(c + 1) * 128], ident_b[:]),
                         reads=[KT, ident_b], writes=[ps[4]], inc=(c == 3))
                m.op(dve, lambda e: e.tensor_copy(KTOK[:].rearrange("p a b -> p (a b)"), p4b[:, 0:512]), reads=[ps[4]], writes=[KTOK])
                if own:
                    ED, UQ, SQ, UH, SG = w_["ED"], w_["UQ"], w_["SQ"], w_["UH"], w_["SG"]
                    m.op(act, lambda e: e.activation(ED[:], DM[:], AF.Exp), reads=[DM], writes=[ED])
                    m.op(act, lambda e: e.activation(UQ[:], ps[1][:], AF.Exp, scale=-1.0), reads=[ps[1]], writes=[UQ])
                    m.op(dve, lambda e: e.tensor_scalar(UQ[:], UQ[:], 1.0, None, ALU.add), reads=[UQ], writes=[UQ])
                    m.op(dve, lambda e: e.reciprocal(UQ[:], UQ[:]), reads=[UQ], writes=[UQ])
                    m.op(dve, lambda e: e.tensor_tensor(SQ[:], ps[1][:], UQ[:], ALU.mult), reads=[ps[1], UQ], writes=[SQ])
                    m.op(dve, lambda e: e.tensor_tensor(QT[:], SQ[:], ED[:], ALU.mult), reads=[SQ, ED], writes=[QT])
                    m.op(act, lambda e: e.activation(UH[:], ps[2][:], AF.Exp, scale=-1.0), reads=[ps[2]], writes=[UH])
                    m.op(dve, lambda e: e.tensor_scalar(UH[:], UH[:], 1.0, None, ALU.add), reads=[UH], writes=[UH])
                    m.op(dve, lambda e: e.reciprocal(UH[:], UH[:]), reads=[UH], writes=[UH])
                    m.op(dve, lambda e: e.tensor_tensor(SG[:], ps[2][:], UH[:], ALU.mult), reads=[ps[2], UH], writes=[SG])
                for c in range(4):
                    cs_ = slice(c * 128, (c + 1) * 128)
                    par = cidx % 2
                    cidx += 1
                    m.op(pe, lambda e, c=c: e.matmul(ps[5][:, 0:128], lhsT=KTOK[:, c, :], rhs=VT[:, c, :], start=True, stop=True),
                         reads=[KTOK, VT], writes=[ps[5]])
                    if own:
                        m.op(pe, lambda e, cs_=cs_: e.matmul(ps[6][:, 0:128], lhsT=KT[:, cs_], rhs=QT[:, cs_], start=True, stop=True),
                             reads=[KT, QT], writes=[ps[6]])
                        m.op(dve, lambda e, par=par: e.tensor_tensor(ASB[par][:], ps[6][:, 0:128], tri_f[:], ALU.mult),
                             reads=[ps[6], tri_f], writes=[ASB[par]])
                        m.op(act, lambda e, c=c, par=par: e.activation(S2B[par][:], Sst[:], AF.Copy, scale=EM[:, c:c + 1]),
                             reads=[Sst, EM], writes=[S2B[par]])
                        m.op(pe, lambda e, c=c, cs_=cs_, par=par: e.matmul(ps[7][:, cs_], lhsT=VT[:, c, :], rhs=ASB[par][:], start=True, stop=False),
                             reads=[VT, ASB[par]], writes=[ps[7]], inc=False)
                        m.op(pe, lambda e, cs_=cs_, par=par: e.matmul(ps[7][:, cs_], lhsT=S2B[par][:], rhs=QT[:, cs_], start=False, stop=True),
                             reads=[S2B[par], QT], writes=[ps[7]])
                    m.op(dve, lambda e, c=c, par=par: e.tensor_scalar(TMP[par][:], ps[5][:, 0:128], ELM[:, c:c + 1], None, ALU.mult),
                         reads=[ps[5], ELM], writes=[TMP[par]])
                    m.op(dve, lambda e, c=c, par=par: e.scalar_tensor_tensor(Sst[:], Sst[:], EL[:, c:c + 1], TMP[par][:], ALU.mult, ALU.add),
                         reads=[Sst, EL, TMP[par]], writes=[Sst])
                if own:
                    SQO, RS, Y = w_["SQO"], w_["RS"], w_["Y"]
                    tok0 = (tg - 4) * 512
                    m.op(act, lambda e: e.activation(SQO[:], ps[7][:], AF.Square), reads=[ps[7]], writes=[SQO])
                    m.op(pe, lambda e: e.matmul(ps[1][:], lhsT=ones_b[:], rhs=SQO[:], start=True, stop=True), reads=[ones_b, SQO], writes=[ps[1]])
                    m.op(act, lambda e: e.activation(RS[:], ps[1][:], AF.Ln, bias=RMS_EPS), reads=[ps[1]], writes=[RS])
                    m.op(act, lambda e: e.activation(RS[:], RS[:], AF.Exp, scale=-0.5), reads=[RS], writes=[RS])
                    m.op(dve, lambda e: e.tensor_tensor(Y[:], ps[7][:], RS[:], ALU.mult), reads=[ps[7], RS], writes=[Y])
                    m.op(dve, lambda e: e.scalar_tensor_tensor(yaT[:, hd, tok0:tok0 + 512], Y[:], nw[:, hd:hd + 1], SG[:], ALU.mult, ALU.mult),
                         reads=[Y, nw, SG], writes=[yaT])
        m.barrier()
        if debug:
            m.dma(sp, dbg["ya"], yaT[:], reads=[yaT])
        if stop_after == "B":
            return _finish(nc, m, [yaT], out_d)

        pc = Bump(AR, OFF_PHASE)
        wsl = [pc.get([128, 4 * 8 * 128], BF16, f"wC{i}", m.new_dsem()) for i in range(2)]
        for b_ in wsl:
            b_.t = b_.t.rearrange("p (j k c) -> p j k c", j=4, k=8)
        CS = [pc.get([128, 2, 512], F32, f"CS{i}", m.new_dsem()) for i in range(2)]
        RA = [pc.get([128, 512], F32, f"RA{i}") for i in range(2)]
        RB = [pc.get([128, 512], F32, f"RB{i}") for i in range(2)]
        KTh = pc.get([128, 2 * T], BF16, "KTh")
        QTh = pc.get([128, T], BF16, "QTh")
        VA = pc.get([128, 32, 129], BF16, "VA")
        KM = pc.get([128, 16], F32, "KM")
        KMB = pc.get([128, 16], BF16, "KMB")
        GB = pc.get([128, 16, 16], F32, "GB", m.new_dsem())
        GM = pc.get([128, 16, 16], F32, "GM")
        TOP = pc.get([128, 16, 8], F32, "TOP")
        THR = pc.get([128, 16], F32, "THR")
        SEL = pc.get([128, 16, 16], F32, "SEL")
        PT = [pc.get([128, 512], BF16, f"PT{i}") for i in range(2)]
        ACC = [pc.get([128, 2, 129], F32, f"ACC{i}") for i in range(2)]
        RC = [pc.get([128, 2], F32, f"RC{i}") for i in range(2)]
        YB = [pc.get([128, 2, 128], BF16, f"YB{i}") for i in range(2)]
        assert pc.off <= AR.nbytes, pc.off
        ps = new_banks()
        psS = [ps[4], ps[5]]
        psO = [ps[6], ps[7]]

        m.dma(sp, GB[:], gbias_d, writes=[GB])
        m.op(dve, lambda e: e.memset(VA[:, :, 128:129], 1.0), writes=[VA])

        def mcols(hd):
            return [4096 + hd * 128, 5120 + hd * 128, 6144 + hd * 128]

        load_w(wsl[0], w_in_d, mcols(0), 3)
        pidx = 0
        ridx = 0
        for hd in range(NH):
            wt = wsl[hd % 2]
            if hd + 1 < NH:
                load_w(wsl[(hd + 1) % 2], w_in_d, mcols(hd + 1), 3)
            for tg in range(8):
                own = tg >= 4
                xt = xT_own if own else xT_pre
                col0 = (tg % 4) * 512
                cs = CS[tg % 2]
                csd = cs_own_d if own else cs_pre_d
                m.dma(sp, cs[:], csd[:, :, col0:col0 + 512], writes=[cs])
                proj_fm(ps[0], wt, 1, xt, col0, 512)
                proj_tm(ps[3], wt, 2, xt, col0)
                if own:
                    proj_fm(ps[1], wt, 0, xt, col0, 512)
                for (bank, dst, dcol) in ([(ps[0], KTh, tg * 512)] + ([(ps[1], QTh, (tg - 4) * 512)] if own else [])):
                    ra, rb = RA[ridx % 2], RB[ridx % 2]
                    ridx += 1
                    m.op(dve, lambda e, bank=bank, ra=ra: e.tensor_tensor(ra[:], bank[:], cs[:, 0, :], ALU.mult), reads=[bank, cs], writes=[ra])
                    m.op(dve, lambda e, bank=bank, rb=rb: e.tensor_tensor(rb[0:64, :], bank[64:128, :], cs[0:64, 1, :], ALU.mult),
                         reads=[bank, cs], writes=[rb])
                    m.op(dve, lambda e, bank=bank, rb=rb: e.tensor_tensor(rb[64:128, :], bank[0:64, :], cs[64:128, 1, :], ALU.mult),
                         reads=[bank, cs], writes=[rb])
                    m.op(pool, lambda e, ra=ra, rb=rb, dst=dst, dcol=dcol: e.tensor_tensor(dst[:, dcol:dcol + 512], ra[:], rb[:], ALU.add),
                         reads=[ra, rb], writes=[dst])
                m.op(act, lambda e: e.activation(VA[:, tg * 4:(tg + 1) * 4, 0:128], ps[3][:].rearrange("p (a b) -> p a b", b=128), AF.Copy),
                     reads=[ps[3]], writes=[VA])
            m.op(dve, lambda e: e.tensor_reduce(KM[:], KTh[:].rearrange("p (n k) -> p n k", k=256), AX.X, ALU.add), reads=[KTh], writes=[KM])
            m.op(dve, lambda e: e.tensor_scalar(KMB[:], KM[:], 1.0 / 256.0, None, ALU.mult), reads=[KM], writes=[KMB])
            for tt in range(16):
                m.op(pe, lambda e, tt=tt: e.matmul(ps[2][:, tt * 16:(tt + 1) * 16], lhsT=QTh[:, tt * 128:(tt + 1) * 128], rhs=KMB[:],
                                                   start=True, stop=True), reads=[QTh, KMB], writes=[ps[2]], inc=(tt == 15))
            m.op(dve, lambda e: e.tensor_tensor(GM[:].rearrange("p a b -> p (a b)"), ps[2][:, 0:256], GB[:].rearrange("p a b -> p (a b)"), ALU.add),
                 reads=[ps[2], GB], writes=[GM])
            for tt in range(16):
                m.op(dve, lambda e, tt=tt: e.max(TOP[:, tt, :], GM[:, tt, :]), reads=[GM], writes=[TOP])
            m.op(dve, lambda e: e.tensor_scalar(THR[:], TOP[:, :, 2], -1.0e29, None, ALU.max), reads=[TOP], writes=[THR])
            m.op(dve, lambda e: e.tensor_tensor(SEL[:], GM[:], THR[:].rearrange("p (a o) -> p a o", o=1).to_broadcast([128, 16, 16]), ALU.is_ge),
                 reads=[GM, THR], writes=[SEL])
            for j in range(8):
                acc, rc, yb = ACC[j % 2], RC[j % 2], YB[j % 2]
                q0 = j * 256
                nown = 8 + j
                sS, sO, pt = psS[pidx % 2], psO[pidx % 2], PT[pidx % 2]
                pidx += 1
                k0 = nown * 256
                m.op(pe, lambda e: e.matmul(sS[:, 0:256], lhsT=KTh[:, k0:k0 + 128], rhs=QTh[:, q0:q0 + 256], start=True, stop=True),
                     reads=[KTh, QTh], writes=[sS], inc=False)
                m.op(pe, lambda e: e.matmul(sS[:, 384:512], lhsT=KTh[:, k0 + 128:k0 + 256], rhs=QTh[:, q0 + 128:q0 + 256], start=True, stop=True),
                     reads=[KTh, QTh], writes=[sS])
                m.op(act, lambda e: e.activation(pt[:, 0:256], sS[:, 0:256], AF.Exp, scale=SCALE), reads=[sS], writes=[pt])
                m.op(act, lambda e: e.activation(pt[:, 384:512], sS[:, 384:512], AF.Exp, scale=SCALE), reads=[sS], writes=[pt])
                m.op(pool, lambda e: e.tensor_tensor(pt[:, 0:128], pt[:, 0:128], tri_b[:], ALU.mult), reads=[pt, tri_b], writes=[pt])
                m.op(pool, lambda e: e.tensor_tensor(pt[:, 384:512], pt[:, 384:512], tri_b[:], ALU.mult), reads=[pt, tri_b], writes=[pt])
                sO3 = sO[:, 0:258].rearrange("p (a b) -> p a b", b=129)
                m.op(pe, lambda e: e.matmul(sO3[:, 0, :], lhsT=pt[:, 0:128], rhs=VA[:, 2 * nown, :], start=True, stop=True),
                     reads=[pt, VA], writes=[sO], inc=False)
                m.op(pe, lambda e: e.matmul(sO3[:, 1, :], lhsT=pt[:, 128:256], rhs=VA[:, 2 * nown, :], start=True, stop=False),
                     reads=[pt, VA], writes=[sO], inc=False)
                m.op(pe, lambda e: e.matmul(sO3[:, 1, :], lhsT=pt[:, 384:512], rhs=VA[:, 2 * nown + 1, :], start=False, stop=True),
                     reads=[pt, VA], writes=[sO])
                m.op(act, lambda e: e.activation(acc[:].rearrange("p a b -> p (a b)"), sO[:, 0:258], AF.Copy), reads=[sO], writes=[acc])
                for n in range(nown):
                    sS, sO, pt = psS[pidx % 2], psO[pidx % 2], PT[pidx % 2]
                    pidx += 1
                    k0 = n * 256
                    for kt in range(2):
                        m.op(pe, lambda e, kt=kt: e.matmul(sS[:, kt * 256:(kt + 1) * 256], lhsT=KTh[:, k0 + kt * 128:k0 + (kt + 1) * 128],
                                                           rhs=QTh[:, q0:q0 + 256], start=True, stop=True),
                             reads=[KTh, QTh], writes=[sS], inc=(kt == 1))
                    m.op(act, lambda e: e.activation(pt[:], sS[:], AF.Exp, scale=SCALE), reads=[sS], writes=[pt])
                    sO3 = sO[:, 0:258].rearrange("p (a b) -> p a b", b=129)
                    for qs in range(2):
                        for kt in range(2):
                            m.op(pe, lambda e, qs=qs, kt=kt: e.matmul(sO3[:, qs, :], lhsT=pt[:, kt * 256 + qs * 128: kt * 256 + (qs + 1) * 128],
                                                                      rhs=VA[:, 2 * n + kt, :], start=(kt == 0), stop=(kt == 1)),
                                 reads=[pt, VA], writes=[sO], inc=(qs == 1 and kt == 1))
                    for qs in range(2):
                        m.op(dve, lambda e, qs=qs: e.scalar_tensor_tensor(acc[:, qs, :], sO3[:, qs, :], SEL[:, 2 * j + qs, n:n + 1], acc[:, qs, :],
                                                                          ALU.mult, ALU.add), reads=[sO, SEL, acc], writes=[acc])
                m.op(dve, lambda e: e.reciprocal(rc[:], acc[:, :, 128]), reads=[acc], writes=[rc])
                for qs in range(2):
                    m.op(dve, lambda e, qs=qs: e.tensor_scalar(yb[:, qs, :], acc[:, qs, 0:128], rc[:, qs:qs + 1], None, ALU.mult),
                         reads=[acc, rc], writes=[yb])
                p2b = ps[2][:].bitcast(BF16)
                for qs in range(2):
                    m.op(pe, lambda e, qs=qs: e.transpose(p2b[:, qs * 128:(qs + 1) * 128], yb[:, qs, :], ident_b[:]),
                         reads=[yb, ident_b], writes=[ps[2]], inc=(qs == 1))
                m.op(act, lambda e: e.activation(ybT[:, hd, q0:q0 + 256], p2b[:, 0:256], AF.Copy), reads=[ps[2]], writes=[ybT])
        m.barrier()
        if debug:
            m.dma(sp, dbg["yb"], ybT[:], reads=[ybT])
        if stop_after == "C":
            return _finish(nc, m, [yaT, ybT], out_d)

        mixT = AR.at(1 * SLOT, [128, 8, T], BF16, "mixT")
        pd = Bump(AR, OFF_PHASE)
        w_o = pd.get([128, 8, D], BF16, "w_o", m.new_dsem())
        off_d2 = pd.off
        wsl = [pd.get([128, 4 * 8 * 128], BF16, f"wD{i}", m.new_dsem()) for i in range(2)]
        for b_ in wsl:
            b_.t = b_.t.rearrange("p (j k c) -> p j k c", j=4, k=8)
        EA = [pd.get([128, 512], F32, f"EA{i}") for i in range(2)]
        EB = [pd.get([128, 512], F32, f"EB{i}") for i in range(2)]
        M1 = [pd.get([128, 512], F32, f"M1{i}") for i in range(2)]
        M2 = [pd.get([128, 512], F32, f"M2{i}") for i in range(2)]
        assert pd.off <= AR.nbytes, pd.off
        ps = new_banks()

        def load_wd(wslot, cc):
            m.dma(pool, wslot[:, 0, :, :], w_a_d[:, :, cc * 128:(cc + 1) * 128], writes=[wslot])
            m.dma(pool, wslot[:, 1, :, :], w_b_d[:, :, cc * 128:(cc + 1) * 128], writes=[wslot])
            m.dma(pool, wslot[:, 2, :, :], w_in_d[:, :, 7168 + cc * 128: 7168 + (cc + 1) * 128], writes=[wslot])
            m.dma(pool, wslot[:, 3, :, :], w_in_d[:, :, 8192 + cc * 128: 8192 + (cc + 1) * 128], writes=[wslot])

        load_wd(wsl[0], 0)
        it = 0
        for cc in range(8):
            wt = wsl[cc % 2]
            if cc + 1 < 8:
                load_wd(wsl[(cc + 1) % 2], cc + 1)
            else:
                for kc in range(8):
                    m.dma(pool, w_o[:, kc, :], w_o_d[:, kc, :], writes=[w_o])
            for tg in range(4):
                col0 = tg * 512
                b0 = (it % 2) * 4
                ea, eb, m1, m2 = EA[it % 2], EB[it % 2], M1[it % 2], M2[it % 2]
                it += 1
                proj_fm(ps[b0 + 0], wt, 0, yaT, col0, 512)
                proj_fm(ps[b0 + 1], wt, 1, ybT, col0, 512)
                proj_fm(ps[b0 + 2], wt, 2, xT_own, col0, 512)
                proj_fm(ps[b0 + 3], wt, 3, xT_own, col0, 512)
                m.op(act, lambda e: e.activation(ea[:], ps[b0 + 2][:], AF.Exp, scale=-1.0, bias=bgn[:, cc:cc + 1]), reads=[ps[b0 + 2], bgn], writes=[ea])
                m.op(act, lambda e: e.activation(eb[:], ps[b0 + 3][:], AF.Exp, scale=-1.0, bias=bgn[:, 8 + cc:9 + cc]), reads=[ps[b0 + 3], bgn], writes=[eb])
                m.op(pool, lambda e: e.tensor_scalar(ea[:], ea[:], 1.0, None, ALU.add), reads=[ea], writes=[ea])
                m.op(pool, lambda e: e.tensor_scalar(eb[:], eb[:], 1.0, None, ALU.add), reads=[eb], writes=[eb])
                m.op(dve, lambda e: e.reciprocal(ea[:], ea[:]), reads=[ea], writes=[ea])
                m.op(dve, lambda e: e.reciprocal(eb[:], eb[:]), reads=[eb], writes=[eb])
                m.op(dve, lambda e: e.tensor_tensor(m1[:], ps[b0 + 0][:], ea[:], ALU.mult), reads=[ps[b0 + 0], ea], writes=[m1])
                m.op(dve, lambda e: e.tensor_tensor(m2[:], ps[b0 + 1][:], eb[:], ALU.mult), reads=[ps[b0 + 1], eb], writes=[m2])
                m.op(pool, lambda e: e.tensor_tensor(mixT[:, cc, col0:col0 + 512], m1[:], m2[:], ALU.add), reads=[m1, m2], writes=[mixT])
        m.barrier()

        x1T = AR.at(3 * SLOT, [128, 8, T], BF16, "x1T")
        pd2 = Bump(AR, off_d2)
        lnp = pd2.get([128, 4, D], F32, "lnp", m.new_dsem())
        XR = [pd2.get([128, D], F32, f"XR{i}", m.new_dsem()) for i in range(2)]
        RR = [pd2.get([128, D], F32, f"RR{i}", m.new_dsem()) for i in range(2)]
        X1B = [pd2.get([128, D], BF16, f"X1B{i}") for i in range(2)]
        STt = [pd2.get([128, 12], F32, f"ST{i}") for i in range(2)]
        MV = [pd2.get([128, 2], F32, f"MV{i}") for i in range(2)]
        RSD = [pd2.get([128, 2], F32, f"RSD{i}") for i in range(2)]
        assert pd2.off <= AR.nbytes, pd2.off
        ps = new_banks()
        m.dma(sp, lnp[:], lnp_d, writes=[lnp])

        def layer_norm(src_banks, xr, rr, st_, mv, rsd, gi):
            for hf in range(2):
                sl = slice(hf * 512, (hf + 1) * 512)
                m.op(dve, lambda e, hf=hf, sl=sl: e.scalar_tensor_tensor(rr[:, sl], xr[:, sl], ALPHA, src_banks[hf][:], ALU.mult, ALU.add),
                     reads=[xr, src_banks[hf]], writes=[rr])
            for hf in range(2):
                m.op(dve, lambda e, hf=hf: e.bn_stats(st_[:, hf * 6:(hf + 1) * 6], rr[:, hf * 512:(hf + 1) * 512]), reads=[rr], writes=[st_])
            m.op(dve, lambda e: e.bn_aggr(mv[:], st_[:]), reads=[st_], writes=[mv])
            m.op(act, lambda e: e.activation(rsd[:, 0:1], mv[:, 1:2], AF.Ln, bias=LN_EPS), reads=[mv], writes=[rsd])
            m.op(act, lambda e: e.activation(rsd[:, 0:1], rsd[:, 0:1], AF.Exp, scale=-0.5), reads=[rsd], writes=[rsd])
            m.op(dve, lambda e: e.scalar_tensor_tensor(rsd[:, 1:2], mv[:, 0:1], -1.0, rsd[:, 0:1], ALU.mult, ALU.mult), reads=[mv, rsd], writes=[rsd])
            m.op(act, lambda e: e.activation(rr[:], rr[:], AF.Identity, scale=rsd[:, 0:1], bias=rsd[:, 1:2]), reads=[rr, rsd], writes=[rr])
            m.op(dve, lambda e: e.tensor_tensor(rr[:], rr[:], lnp[:, gi, :], ALU.mult), reads=[rr, lnp], writes=[rr])
            m.op(pool, lambda e: e.tensor_tensor(rr[:], rr[:], lnp[:, gi + 1, :], ALU.add), reads=[rr, lnp], writes=[rr])

        for tt in range(16):
            xr, rr, x1b, st_, mv, rsd = XR[tt % 2], RR[tt % 2], X1B[tt % 2], STt[tt % 2], MV[tt % 2], RSD[tt % 2]
            b0 = (tt % 2) * 4
            tsl = slice(tt * 128, (tt + 1) * 128)
            m.dma(sp, xr[:], x_tok_d[tsl, :], writes=[xr])
            for hf in range(2):
                for cc in range(8):
                    m.op(pe, lambda e, hf=hf, cc=cc: e.matmul(ps[b0 + hf][:], lhsT=mixT[:, cc, tsl], rhs=w_o[:, cc, hf * 512:(hf + 1) * 512],
                                                              start=(cc == 0), stop=(cc == 7)),
                         reads=[mixT, w_o], writes=[ps[b0 + hf]], inc=(cc == 7))
            layer_norm([ps[b0], ps[b0 + 1]], xr, rr, st_, mv, rsd, 0)
            m.dma(sp, x1_d[tsl, :], rr[:], reads=[rr])
            if debug:
                m.dma(sp, dbg["x1"][tsl, :], rr[:], reads=[rr])
            m.op(act, lambda e: e.activation(x1b[:], rr[:], AF.Copy), reads=[rr], writes=[x1b])
            pb_ = ps[b0 + 2][:].bitcast(BF16)
            for kc in range(8):
                m.op(pe, lambda e, kc=kc: e.transpose(pb_[:, kc * 128:(kc + 1) * 128], x1b[:, kc * 128:(kc + 1) * 128], ident_b[:]),
                     reads=[x1b, ident_b], writes=[ps[b0 + 2]], inc=(kc == 7))
            m.op(dve, lambda e: e.tensor_copy(x1T[:, :, tsl], pb_.rearrange("p (k t) -> p k t", t=128)), reads=[ps[b0 + 2]], writes=[x1T])
        m.barrier()
        if stop_after == "D":
            return _finish(nc, m, [], out_d)

        aT = AR.at(0, [128, NFC, 1024], BF16, "aT")
        w_fd = AR.at(45056, [128, NFC, D], BF16, "w_fd", m.new_dsem())
        assert 45056 + NFC * D * 2 <= 3 * SLOT
        pe_ = Bump(AR, OFF_PHASE)
        wsl = [pe_.get([128, 4 * 8 * 128], BF16, f"wE{i}", m.new_dsem()) for i in range(2)]
        for b_ in wsl:
            b_.t = b_.t.rearrange("p (j k c) -> p j k c", j=4, k=8)
        EG = [pe_.get([128, 512], F32, f"EG{i}") for i in range(2)]
        SGf = [pe_.get([128, 512], F32, f"SGf{i}") for i in range(2)]
        lnp = pe_.get([128, 4, D], F32, "lnp2", m.new_dsem())
        XR = [pe_.get([128, D], F32, f"XRe{i}", m.new_dsem()) for i in range(2)]
        RR = [pe_.get([128, D], F32, f"RRe{i}", m.new_dsem()) for i in range(2)]
        STt = [pe_.get([128, 12], F32, f"STe{i}") for i in range(2)]
        MV = [pe_.get([128, 2], F32, f"MVe{i}") for i in range(2)]
        RSD = [pe_.get([128, 2], F32, f"RSDe{i}") for i in range(2)]
        assert pe_.off <= AR.nbytes, pe_.off
        ps = new_banks()
        m.dma(sp, lnp[:], lnp_d, writes=[lnp])
        for fc in range(NFC):
            m.dma(pool, w_fd[:, fc, :], w_fd_d[:, fc, :], writes=[w_fd])

        def load_we(wslot, fc):
            m.dma(pool, wslot[:, 0, :, :], w_fi_d[:, :, fc * 128:(fc + 1) * 128], writes=[wslot])
            m.dma(pool, wslot[:, 1, :, :], w_fi_d[:, :, DFF + fc * 128: DFF + (fc + 1) * 128], writes=[wslot])

        it = 0
        wi = 0
        for half in range(2):
            load_we(wsl[wi % 2], 0)
            for fc in range(NFC):
                wt = wsl[wi % 2]
                wi += 1
                if fc + 1 < NFC:
                    load_we(wsl[wi % 2], fc + 1)
                for tg in range(2):
                    col0 = half * 1024 + tg * 512
                    b0 = (it % 2) * 2
                    eg, sg = EG[it % 2], SGf[it % 2]
                    it += 1
                    proj_fm(ps[b0], wt, 0, x1T, col0, 512)
                    proj_fm(ps[b0 + 1], wt, 1, x1T, col0, 512)
                    m.op(act, lambda e: e.activation(eg[:], ps[b0][:], AF.Exp, scale=-1.0), reads=[ps[b0]], writes=[eg])
                    m.op(pool, lambda e: e.tensor_scalar(eg[:], eg[:], 1.0, None, ALU.add), reads=[eg], writes=[eg])
                    m.op(dve, lambda e: e.reciprocal(eg[:], eg[:]), reads=[eg], writes=[eg])
                    m.op(dve, lambda e: e.tensor_tensor(sg[:], ps[b0][:], eg[:], ALU.mult), reads=[ps[b0], eg], writes=[sg])
                    m.op(dve, lambda e: e.tensor_tensor(aT[:, fc, tg * 512:(tg + 1) * 512], ps[b0 + 1][:], sg[:], ALU.mult),
                         reads=[ps[b0 + 1], sg], writes=[aT])
            for t8 in range(8):
                tt = half * 8 + t8
                xr, rr, st_, mv, rsd = XR[tt % 2], RR[tt % 2], STt[tt % 2], MV[tt % 2], RSD[tt % 2]
                b0 = 4 + (tt % 2) * 2
                tsl = slice(tt * 128, (tt + 1) * 128)
                lsl = slice(t8 * 128, (t8 + 1) * 128)
                m.dma(sp, xr[:], x1_d[tsl, :], writes=[xr])
                for hf in range(2):
                    for fc in range(NFC):
                        m.op(pe, lambda e, hf=hf, fc=fc: e.matmul(ps[b0 + hf][:], lhsT=aT[:, fc, lsl], rhs=w_fd[:, fc, hf * 512:(hf + 1) * 512],
                                                                  start=(fc == 0), stop=(fc == NFC - 1)),
                             reads=[aT, w_fd], writes=[ps[b0 + hf]], inc=(fc == NFC - 1))
                layer_norm([ps[b0], ps[b0 + 1]], xr, rr, st_, mv, rsd, 2)
                m.dma(sp, out_d[tsl, :], rr[:], reads=[rr])
        m.barrier()
        return _finish(nc, m, [], out_d)


def _finish(nc, m, bufs, out_d):
    m.barrier()
    return nc


def _r_w(w):
    K, N = w.shape
    return np.ascontiguousarray(w.reshape(K // 128, 128, N).transpose(1, 0, 2))


def _rope_tables(pos):
    half = 64
    inv_freq = (10000.0 ** (-np.arange(half, dtype=np.float32) / np.float32(half))).astype(np.float32)
    ang = (pos.astype(np.float32)[:, None] * inv_freq[None, :]).astype(np.float32)
    cos = np.cos(ang).astype(np.float32).T
    sin = np.sin(ang).astype(np.float32).T
    cs = np.empty((128, 2, pos.shape[0]), np.float32)
    cs[0:64, 0] = cos
    cs[64:128, 0] = cos
    cs[0:64, 1] = -sin
    cs[64:128, 1] = sin
    return cs


def make_in_maps(x, w_in, lb_logits, hgrn_norm_w, w_branch_a, w_branch_b, b_gate, w_out,
                 ln1_w, ln1_b, w_ffn_in, w_ffn_down, ln2_w, ln2_b):
    x = np.asarray(x, np.float32)
    f = lambda a: np.asarray(a, np.float32)
    shared = {
        "w_in_r": _r_w(f(w_in)[0]),
        "w_a_r": _r_w(f(w_branch_a)[0]),
        "w_b_r": _r_w(f(w_branch_b)[0]),
        "w_o_r": _r_w(f(w_out)[0]),
        "w_fi_r": _r_w(f(w_ffn_in)[0]),
        "w_fd_r": _r_w(f(w_ffn_down)[0]),
        "lbl": np.ascontiguousarray(f(lb_logits).reshape(2, 8, 128).transpose(2, 0, 1)),
        "nw": np.ascontiguousarray(f(hgrn_norm_w)[0].reshape(8, 128).T),
        "bg": np.ascontiguousarray(f(b_gate)[0].reshape(16, 128).T),
        "lnp": np.ascontiguousarray(np.broadcast_to(
            np.stack([f(ln1_w)[0], f(ln1_b)[0], f(ln2_w)[0], f(ln2_b)[0]])[None], (128, 4, D))),
        "tri": np.triu(np.ones((128, 128), np.float32)),
        "ident": np.eye(128, dtype=np.float32),
    }
    sm = np.ones((128, 512), np.float32)
    sm[:, 0::128] = 0.0
    shared["scanm"] = sm
    in_maps = []
    for c in range(8):
        b, h = c // 2, c % 2
        xo = x[b, h * T:(h + 1) * T, :]
        xp = x[b, 0:T, :] if h == 1 else np.zeros((T, D), np.float32)
        mp = dict(shared)
        mp["x_tok"] = np.ascontiguousarray(xo)
        mp["xT_own"] = np.ascontiguousarray(xo.T.reshape(8, 128, T).transpose(1, 0, 2))
        mp["xT_pre"] = np.ascontiguousarray(xp.T.reshape(8, 128, T).transpose(1, 0, 2))
        mp["cs_own"] = _rope_tables(np.arange(h * T, (h + 1) * T))
        mp["cs_pre"] = _rope_tables(np.arange(0, T))
        gb = np.full((16, 16), NEG, np.float32)
        for tt in range(16):
            j = tt // 2
            for n in range(16):
                if n < 8 + j and (h == 1 or n >= 8):
                    gb[tt, n] = 0.0
        mp["gbias"] = np.ascontiguousarray(np.broadcast_to(gb[None], (128, 16, 16)))
        in_maps.append(mp)
    return in_maps


_NC_CACHE = {}


def kernel(**inputs):
    in_maps = make_in_maps(**inputs)
    if "nc" not in _NC_CACHE:
        _NC_CACHE["nc"] = build_program()
    nc = _NC_CACHE["nc"]
    res = run_bass_kernel_spmd(nc, in_maps, core_ids=list(range(8)))
    out = np.empty((4, 2 * T, D), np.float32)
    for c in range(8):
        b, h = c // 2, c % 2
        out[b, h * T:(h + 1) * T, :] = np.asarray(res.results[c]["out"], np.float32)
    return out
```

```python
import numpy as np
from contextlib import ExitStack
import concourse.bass as bass
import concourse.mybir as mybir
from concourse.bass_utils import run_bass_kernel_spmd

F32 = mybir.dt.float32
BF16 = mybir.dt.bfloat16
ALU = mybir.AluOpType
AF = mybir.ActivationFunctionType
AX = mybir.AxisListType

D = 1024
T = 2048
NH = 8
DFF = 2816
NFC = DFF // 128
ALPHA = 2.0 ** 0.25
LN_EPS = 1e-5
RMS_EPS = 1e-6
SCALE = 128.0 ** -0.5
NEG = -1.0e30


class Eng:
    def __init__(self, name, e, sem):
        self.name = name
        self.e = e
        self.sem = sem
        self.cnt = 0
        self.seen = {}


class DmaSem:
    def __init__(self, sem):
        self.sem = sem
        self.cnt = 0
        self.name = "dma"


class Buf:
    __slots__ = ("t", "w", "r", "dsem", "name", "excl")

    def __init__(self, t, name=""):
        self.excl = False
        self.t = t
        self.w = None
        self.r = {}
        self.dsem = None
        self.name = name

    def __getitem__(self, k):
        return self.t[k]


class MK:
    def __init__(self, nc, stack):
        self.nc = nc
        self.stack = stack
        self.n_sem = 0
        self.dsems = []
        self.pe = self._eng("pe", nc.tensor)
        self.act = self._eng("act", nc.scalar)
        self.dve = self._eng("dve", nc.vector)
        self.pool = self._eng("pool", nc.gpsimd)
        self.sp = self._eng("sp", nc.sync)
        self.engs = [self.pe, self.act, self.dve, self.pool, self.sp]
        self.n_inst = 0

    def _sem(self, name):
        self.n_sem += 1
        return self.stack.enter_context(self.nc.semaphore(f"{name}_{self.n_sem}"))

    def _eng(self, name, e):
        return Eng(name, e, self._sem("s_" + name))

    def new_dsem(self, name="d"):
        d = DmaSem(self._sem(name))
        self.dsems.append(d)
        return d

    def buf(self, ap, name="", dsem=None):
        b = Buf(ap, name)
        b.dsem = dsem
        return b

    def _need(self, eng, need, src, v):
        if src is eng and eng is self.pe:
            return
        if eng.seen.get(src, 0) >= v:
            return
        if need.get(src, 0) < v:
            need[src] = v

    def _waits(self, eng, reads, writes):
        need = {}
        for b in reads:
            if b.w is not None:
                self._need(eng, need, *b.w)
            if b.excl:
                for src, v in b.r.items():
                    if src is not eng:
                        self._need(eng, need, src, v)
        for b in writes:
            if b.w is not None:
                self._need(eng, need, *b.w)
            for src, v in b.r.items():
                self._need(eng, need, src, v)
        for src, v in need.items():
            eng.e.wait_ge(src.sem, v)
            eng.seen[src] = v

    def _mark(self, d, reads, writes):
        src, v = d
        for b in reads:
            if b.r.get(src, 0) < v:
                b.r[src] = v
        for b in writes:
            b.w = d
            b.r = {}

    def op(self, eng, fn, reads=(), writes=(), inc=True):
        self._waits(eng, reads, writes)
        ins = fn(eng.e)
        self.n_inst += 1
        if inc:
            ins.then_inc(eng.sem, 1)
            eng.cnt += 1
            d = (eng, eng.cnt)
        else:
            d = (eng, eng.cnt + 1)
        self._mark(d, reads, writes)
        return d

    def dma(self, q, out_ap, in_ap, reads=(), writes=(), **kw):
        self._waits(q, reads, writes)
        tgt = None
        for b in list(writes) + list(reads):
            if b.dsem is not None:
                tgt = b.dsem
                break
        assert tgt is not None
        ins = q.e.dma_start(out=out_ap, in_=in_ap, **kw)
        ins.then_inc(tgt.sem, 16)
        tgt.cnt += 16
        d = (tgt, tgt.cnt)
        self.n_inst += 1
        self._mark(d, reads, writes)
        return d

    def barrier(self):
        srcs = [e for e in self.engs if e.cnt > 0] + [d for d in self.dsems if d.cnt > 0]
        for e in self.engs:
            for s in srcs:
                if e.seen.get(s, 0) < s.cnt:
                    e.e.wait_ge(s.sem, s.cnt)
                    e.seen[s] = s.cnt


class Arena:
    def __init__(self, m, nbytes):
        self.m = m
        self.nbytes = nbytes
        self.t = m.stack.enter_context(m.nc.sbuf_tensor("arena", [128, nbytes // 2], BF16))
        self.top = 0

    def at(self, off, shape, dtype, name="", dsem=None):
        n = 1
        for s in shape[1:]:
            n *= s
        esz = 2 if dtype == BF16 else 4
        assert off % 4 == 0 and off + n * esz <= self.nbytes, (name, off, n * esz, self.nbytes)
        ap = self.t[:, off // 2: off // 2 + n * esz // 2]
        if dtype != BF16:
            ap = ap.bitcast(dtype)
        if len(shape) == 3:
            ap = ap.rearrange("p (a b) -> p a b", b=shape[2])
        elif len(shape) == 4:
            ap = ap.rearrange("p (a b c) -> p a b c", b=shape[2], c=shape[3])
        return self.m.buf(ap, name, dsem)


class Bump:
    def __init__(self, arena, start):
        self.a = arena
        self.off = start

    def get(self, shape, dtype, name="", dsem=None):
        n = 1
        for s in shape[1:]:
            n *= s
        esz = 2 if dtype == BF16 else 4
        b = self.a.at(self.off, shape, dtype, name, dsem)
        self.off += (n * esz + 31) // 32 * 32
        return b


def build_program(stop_after=None, debug=False, nhB=NH, tgsB=tuple(range(8))):
    nc = bass.Bass("TRN2", target_bir_lowering=False)

    def din(name, shape):
        return nc.dram_tensor(name, list(shape), F32, kind="ExternalInput").ap()

    xT_own_d = din("xT_own", [128, 8, T])
    xT_pre_d = din("xT_pre", [128, 8, T])
    x_tok_d = din("x_tok", [T, D])
    w_hg_d = din("w_hg", [128, NH, 4 * 1024])
    w_mb_d = din("w_mb", [128, NH, 3 * 1024])
    w_d1_d = din("w_d1", [128, 8, 4 * 1024])
    w_e1_d = din("w_e1", [128, NFC, 2 * 1024])
    w_o_d = din("w_o_r", [128, 8, D])
    w_fd_d = din("w_fd_r", [128, NFC, D])
    lbl_d = din("lbl", [128, 2, 8])
    nw_d = din("nw", [128, 8])
    bg_d = din("bg", [128, 16])
    lnp_d = din("lnp", [128, 4, D])
    cs_own_d = din("cs_own", [128, 2, T])
    cs_pre_d = din("cs_pre", [128, 2, T])
    tri_d = din("tri", [128, 128])
    ident_d = din("ident", [128, 128])
    scanm_d = din("scanm", [128, 512])
    gbias_d = din("gbias", [128, 16, 16])
    out_d = nc.dram_tensor("out", [T, D], F32, kind="ExternalOutput").ap()
    x1_d = nc.dram_tensor("x1_scratch", [T, D], F32, kind="Internal").ap()
    dbg = {}
    if debug:
        dbg["ya"] = nc.dram_tensor("dbg_ya", [128, 8, T], BF16, kind="ExternalOutput").ap()
        dbg["yb"] = nc.dram_tensor("dbg_yb", [128, 8, T], BF16, kind="ExternalOutput").ap()
        dbg["x1"] = nc.dram_tensor("dbg_x1", [T, D], F32, kind="ExternalOutput").ap()

    with ExitStack() as st:
        m = MK(nc, st)
        pe, act, dve, pool, sp = m.pe, m.act, m.dve, m.pool, m.sp
        PSt = st.enter_context(nc.psum_tensor("ps", [128, 8, 512], F32))
        AR = Arena(m, 207 * 1024)

        def new_banks():
            bs = [m.buf(PSt[:, i, :], f"ps{i}") for i in range(8)]
            for b_ in bs:
                b_.excl = True
            return bs

        SLOT = 8 * T * 2
        OFF_CONST = 4 * SLOT
        cb = Bump(AR, OFF_CONST)
        tri_f = cb.get([128, 128], F32, "tri_f", m.new_dsem())
        tri_b = cb.get([128, 128], BF16, "tri_b")
        ident_b = cb.get([128, 128], BF16, "ident_b", m.new_dsem())
        ones_b = cb.get([128, 128], BF16, "ones_b")
        scanm = cb.get([128, 512], F32, "scanm", m.new_dsem())
        lbl = cb.get([128, 2, 8], F32, "lbl", m.new_dsem())
        lbv = cb.get([128, 8], F32, "lbv")
        omlb = cb.get([128, 8], F32, "omlb")
        ln1mlb = cb.get([128, 8], F32, "ln1mlb")
        lbt = cb.get([128, 8], F32, "lbt")
        nw = cb.get([128, 8], F32, "nw", m.new_dsem())
        bgn = cb.get([128, 16], F32, "bgn", m.new_dsem())
        OFF_PHASE = cb.off

        xT_own = AR.at(0 * SLOT, [128, 8, T], BF16, "xT_own", m.new_dsem())
        xT_pre = AR.at(1 * SLOT, [128, 8, T], BF16, "xT_pre", m.new_dsem())
        ybT = AR.at(2 * SLOT, [128, 8, T], BF16, "ybT", m.new_dsem())
        yaT = AR.at(3 * SLOT, [128, 8, T], BF16, "yaT", m.new_dsem())

        m.dma(sp, tri_f[:], tri_d, writes=[tri_f])
        m.dma(pool, ident_b[:], ident_d, writes=[ident_b])
        m.dma(sp, scanm[:], scanm_d, writes=[scanm])
        m.dma(sp, lbl[:], lbl_d, writes=[lbl])
        m.dma(sp, nw[:], nw_d, writes=[nw])
        m.dma(sp, bgn[:], bg_d, writes=[bgn])
        for kc in range(8):
            m.dma(pool, xT_pre[:, kc, :], xT_pre_d[:, kc, :], writes=[xT_pre])
        for kc in range(8):
            m.dma(pool, xT_own[:, kc, :], xT_own_d[:, kc, :], writes=[xT_own])
        m.op(dve, lambda e: e.tensor_copy(tri_b[:], tri_f[:]), reads=[tri_f], writes=[tri_b])
        m.op(dve, lambda e: e.memset(ones_b[:], 1.0 / 128.0), writes=[ones_b])
        m.op(dve, lambda e: e.tensor_tensor(lbt[:], lbl[:, 1, :], lbl[:, 0, :], ALU.subtract), reads=[lbl], writes=[lbt])
        m.op(act, lambda e: e.activation(lbt[:], lbt[:], AF.Exp), reads=[lbt], writes=[lbt])
        m.op(dve, lambda e: e.tensor_scalar(lbv[:], lbt[:], 1.0, None, ALU.add), reads=[lbt], writes=[lbv])
        m.op(dve, lambda e: e.reciprocal(lbv[:], lbv[:]), reads=[lbv], writes=[lbv])
        m.op(dve, lambda e: e.tensor_tensor(omlb[:], lbt[:], lbv[:], ALU.mult), reads=[lbt, lbv], writes=[omlb])
        m.op(act, lambda e: e.activation(ln1mlb[:], omlb[:], AF.Ln), reads=[omlb], writes=[ln1mlb])
        m.op(dve, lambda e: e.tensor_scalar(bgn[:], bgn[:], -1.0, None, ALU.mult), reads=[bgn], writes=[bgn])

        if stop_after == "0":
            return _finish(nc, m, [], out_d)

        def proj_fm(bank, wt, j, xt, col0, ncol):
            for kc in range(8):
                m.op(pe, lambda e, kc=kc: e.matmul(bank[:, 0:ncol], lhsT=wt[:, j, kc, :], rhs=xt[:, kc, col0:col0 + ncol],
                                                   start=(kc == 0), stop=(kc == 7)),
                     reads=[wt, xt], writes=[bank], inc=(kc == 7))

        def proj_tm(bank, wt, j, xt, col0):
            for tt in range(4):
                for kc in range(8):
                    m.op(pe, lambda e, kc=kc, tt=tt: e.matmul(bank[:, tt * 128:(tt + 1) * 128],
                                                              lhsT=xt[:, kc, col0 + tt * 128: col0 + (tt + 1) * 128],
                                                              rhs=wt[:, j, kc, :], start=(kc == 0), stop=(kc == 7)),
                         reads=[wt, xt], writes=[bank], inc=(kc == 7 and tt == 3))

        def load_w(wslot, src_d, idx, ntile):
            flat = wslot.t.rearrange("p j k c -> p (j k c)")
            for j0 in range(0, ntile, 2):
                n = min(2, ntile - j0) * 1024
                m.dma(pool, flat[:, j0 * 1024: j0 * 1024 + n], src_d[:, idx, j0 * 1024: j0 * 1024 + n], writes=[wslot])

        pb = Bump(AR, OFF_PHASE)
        wsl = [pb.get([128, 4 * 8 * 128], BF16, f"wB{i}", m.new_dsem()) for i in range(2)]
        for b_ in wsl:
            b_.t = b_.t.rearrange("p (j k c) -> p j k c", j=4, k=8)
        NSET = 2
        W = [{} for _ in range(NSET)]
        for s_ in range(NSET):
            for nm in ["U", "L1", "L2", "G", "WT", "DM", "EDN", "UQ", "UH", "SG"]:
                W[s_][nm] = pb.get([128, 512], F32, f"{nm}{s_}")
            W[s_]["ED"] = W[s_]["U"]
            W[s_]["SQ"] = W[s_]["L2"]
            W[s_]["RS"] = W[s_]["L1"]
            W[s_]["Y"] = W[s_]["EDN"]
            for nm in ["KT", "QT", "SQO"]:
                W[s_][nm] = pb.get([128, 512], BF16, f"{nm}{s_}")
            W[s_]["VT"] = pb.get([128, 4, 128], BF16, f"VT{s_}")
            W[s_]["KTOK"] = pb.get([128, 4, 128], BF16, f"KTOK{s_}")
            W[s_]["EM"] = pb.get([128, 4], F32, f"EM{s_}")
            W[s_]["EL"] = pb.get([128, 4], F32, f"EL{s_}")
            W[s_]["ELM"] = pb.get([128, 4], F32, f"ELM{s_}")
        Sst = pb.get([128, 128], F32, "Sst")
        S2B = [pb.get([128, 128], BF16, f"S2B{i}") for i in range(2)]
        ASB = [pb.get([128, 128], BF16, f"ASB{i}") for i in range(2)]
        TMP = [pb.get([128, 128], F32, f"TMP{i}") for i in range(2)]
        assert pb.off <= AR.nbytes, pb.off
        ps = new_banks()

        load_w(wsl[0], w_hg_d, 0, 4)
        cidx = 0
        for hd in range(nhB):
            wt = wsl[hd % 2]
            if hd + 1 < nhB:
                load_w(wsl[(hd + 1) % 2], w_hg_d, hd + 1, 4)
            m.op(dve, lambda e: e.memset(Sst[:], 0.0), writes=[Sst])
            for tg in tgsB:
                own = tg >= 4
                xt = xT_own if own else xT_pre
                col0 = (tg % 4) * 512
                w_ = W[tg % NSET]
                proj_fm(ps[0], wt, 1, xt, col0, 512)
                proj_tm(ps[3], wt, 2, xt, col0)
                if own:
                    proj_fm(ps[1], wt, 0, xt, col0, 512)
                    proj_fm(ps[2], wt, 3, xt, col0, 512)
                U, L1, L2, G, WT, DM, EDN = w_["U"], w_["L1"], w_["L2"], w_["G"], w_["WT"], w_["DM"], w_["EDN"]
                KT, QT, VT, KTOK = w_["KT"], w_["QT"], w_["VT"], w_["KTOK"]
                EM, EL, ELM = w_["EM"], w_["EL"], w_["ELM"]
                m.op(act, lambda e: e.activation(U[:], ps[0][:], AF.Exp, scale=-1.0), reads=[ps[0]], writes=[U])
                m.op(act, lambda e: e.activation(L1[:], U[:], AF.Ln, bias=1.0), reads=[U], writes=[L1])
                m.op(act, lambda e: e.activation(L2[:], U[:], AF.Ln, scale=lbv[:, hd:hd + 1], bias=1.0), reads=[U, lbv], writes=[L2])
                m.op(dve, lambda e: e.tensor_tensor(L2[:], L2[:], L1[:], ALU.subtract), reads=[L2, L1], writes=[L2])
                m.op(dve, lambda e: e.tensor_tensor_scan(G[:], scanm[:], L2[:], 0.0, ALU.mult, ALU.add), reads=[scanm, L2], writes=[G])
                m.op(dve, lambda e: e.scalar_tensor_tensor(WT[:], ps[0][:], -1.0, L1[:], ALU.mult, ALU.subtract), reads=[ps[0], L1], writes=[WT])
                m.op(act, lambda e: e.activation(WT[:], WT[:], AF.Exp, bias=ln1mlb[:, hd:hd + 1]), reads=[WT, ln1mlb], writes=[WT])
                G3 = G[:].rearrange("p (c t) -> p c t", t=128)
                D3 = DM[:].rearrange("p (c t) -> p c t", t=128)
                m.op(dve, lambda e: e.tensor_tensor(D3, G3, G3[:, :, 63:64].to_broadcast([128, 4, 128]), ALU.subtract), reads=[G], writes=[DM])
                m.op(act, lambda e: e.activation(EDN[:], DM[:], AF.Exp, scale=-1.0), reads=[DM], writes=[EDN])
                m.op(dve, lambda e: e.tensor_tensor(KT[:], WT[:], EDN[:], ALU.mult), reads=[WT, EDN], writes=[KT])
                m.op(act, lambda e: e.activation(EM[:], G3[:, :, 63], AF.Exp), reads=[G], writes=[EM])
                m.op(act, lambda e: e.activation(EL[:], G3[:, :, 127], AF.Exp), reads=[G], writes=[EL])
                m.op(dve, lambda e: e.tensor_tensor(ELM[:], G3[:, :, 127], G3[:, :, 63], ALU.subtract), reads=[G], writes=[ELM])
                m.op(act, lambda e: e.activation(ELM[:], ELM[:], AF.Exp), reads=[ELM], writes=[ELM])
                m.op(act, lambda e: e.activation(VT[:].rearrange("p a b -> p (a b)"), ps[3][:], AF.Copy), reads=[ps[3]], writes=[VT])
                p4b = ps[4][:].bitcast(BF16)
                for c in range(4):
                    m.op(pe, lambda e, c=c: e.transpose(p4b[:, c * 128:(c + 1) * 128], KT[:, c * 128:(c + 1) * 128], ident_b[:]),
                         reads=[KT, ident_b], writes=[ps[4]], inc=(c == 3))
                m.op(dve, lambda e: e.tensor_copy(KTOK[:].rearrange("p a b -> p (a b)"), p4b[:, 0:512]), reads=[ps[4]], writes=[KTOK])
                if own:
                    ED, UQ, SQ, UH, SG = w_["ED"], w_["UQ"], w_["SQ"], w_["UH"], w_["SG"]
                    m.op(act, lambda e: e.activation(ED[:], DM[:], AF.Exp), reads=[DM], writes=[ED])
                    m.op(act, lambda e: e.activation(UQ[:], ps[1][:], AF.Exp, scale=-1.0), reads=[ps[1]], writes=[UQ])
                    m.op(dve, lambda e: e.tensor_scalar(UQ[:], UQ[:], 1.0, None, ALU.add), reads=[UQ], writes=[UQ])
                    m.op(dve, lambda e: e.reciprocal(UQ[:], UQ[:]), reads=[UQ], writes=[UQ])
                    m.op(dve, lambda e: e.tensor_tensor(SQ[:], ps[1][:], UQ[:], ALU.mult), reads=[ps[1], UQ], writes=[SQ])
                    m.op(dve, lambda e: e.tensor_tensor(QT[:], SQ[:], ED[:], ALU.mult), reads=[SQ, ED], writes=[QT])
                    m.op(act, lambda e: e.activation(UH[:], ps[2][:], AF.Exp, scale=-1.0), reads=[ps[2]], writes=[UH])
                    m.op(dve, lambda e: e.tensor_scalar(UH[:], UH[:], 1.0, None, ALU.add), reads=[UH], writes=[UH])
                    m.op(dve, lambda e: e.reciprocal(UH[:], UH[:]), reads=[UH], writes=[UH])
                    m.op(dve, lambda e: e.tensor_tensor(SG[:], ps[2][:], UH[:], ALU.mult), reads=[ps[2], UH], writes=[SG])
                for c in range(4):
                    cs_ = slice(c * 128, (c + 1) * 128)
                    par = cidx % 2
                    cidx += 1
                    m.op(pe, lambda e, c=c: e.matmul(ps[5][:, 0:128], lhsT=KTOK[:, c, :], rhs=VT[:, c, :], start=True, stop=True),
                         reads=[KTOK, VT], writes=[ps[5]])
                    if own:
                        m.op(pe, lambda e, cs_=cs_: e.matmul(ps[6][:, 0:128], lhsT=KT[:, cs_], rhs=QT[:, cs_], start=True, stop=True),
                             reads=[KT, QT], writes=[ps[6]])
                        m.op(dve, lambda e, par=par: e.tensor_tensor(ASB[par][:], ps[6][:, 0:128], tri_f[:], ALU.mult),
                             reads=[ps[6], tri_f], writes=[ASB[par]])
                        m.op(act, lambda e, c=c, par=par: e.activation(S2B[par][:], Sst[:], AF.Copy, scale=EM[:, c:c + 1]),
                             reads=[Sst, EM], writes=[S2B[par]])
                        m.op(pe, lambda e, c=c, cs_=cs_, par=par: e.matmul(ps[7][:, cs_], lhsT=VT[:, c, :], rhs=ASB[par][:], start=True, stop=False),
                             reads=[VT, ASB[par]], writes=[ps[7]], inc=False)
                        m.op(pe, lambda e, cs_=cs_, par=par: e.matmul(ps[7][:, cs_], lhsT=S2B[par][:], rhs=QT[:, cs_], start=False, stop=True),
                             reads=[S2B[par], QT], writes=[ps[7]])
                    m.op(dve, lambda e, c=c, par=par: e.tensor_scalar(TMP[par][:], ps[5][:, 0:128], ELM[:, c:c + 1], None, ALU.mult),
                         reads=[ps[5], ELM], writes=[TMP[par]])
                    m.op(dve, lambda e, c=c, par=par: e.scalar_tensor_tensor(Sst[:], Sst[:], EL[:, c:c + 1], TMP[par][:], ALU.mult, ALU.add),
                         reads=[Sst, EL, TMP[par]], writes=[Sst])
                if own:
                    SQO, RS, Y = w_["SQO"], w_["RS"], w_["Y"]
                    tok0 = (tg - 4) * 512
                    m.op(act, lambda e: e.activation(SQO[:], ps[7][:], AF.Square), reads=[ps[7]], writes=[SQO])
                    m.op(pe, lambda e: e.matmul(ps[1][:], lhsT=ones_b[:], rhs=SQO[:], start=True, stop=True), reads=[ones_b, SQO], writes=[ps[1]])
                    m.op(act, lambda e: e.activation(RS[:], ps[1][:], AF.Ln, bias=RMS_EPS), reads=[ps[1]], writes=[RS])
                    m.op(act, lambda e: e.activation(RS[:], RS[:], AF.Exp, scale=-0.5), reads=[RS], writes=[RS])
                    m.op(dve, lambda e: e.tensor_tensor(Y[:], ps[7][:], RS[:], ALU.mult), reads=[ps[7], RS], writes=[Y])
                    m.op(dve, lambda e: e.scalar_tensor_tensor(yaT[:, hd, tok0:tok0 + 512], Y[:], nw[:, hd:hd + 1], SG[:], ALU.mult, ALU.mult),
                         reads=[Y, nw, SG], writes=[yaT])
        m.barrier()
        if debug:
            m.dma(sp, dbg["ya"], yaT[:], reads=[yaT])
        if stop_after == "B":
            return _finish(nc, m, [yaT], out_d)

        pc = Bump(AR, OFF_PHASE)
        wsl = [pc.get([128, 4 * 8 * 128], BF16, f"wC{i}", m.new_dsem()) for i in range(2)]
        for b_ in wsl:
            b_.t = b_.t.rearrange("p (j k c) -> p j k c", j=4, k=8)
        CS = [pc.get([128, 2, 512], F32, f"CS{i}", m.new_dsem()) for i in range(2)]
        RA = [pc.get([128, 512], F32, f"RA{i}") for i in range(2)]
        RB = [pc.get([128, 512], F32, f"RB{i}") for i in range(2)]
        KTh = pc.get([128, 2 * T], BF16, "KTh")
        QTh = pc.get([128, T], BF16, "QTh")
        VA = pc.get([128, 32, 129], BF16, "VA")
        KM = pc.get([128, 16], F32, "KM")
        KMB = pc.get([128, 16], BF16, "KMB")
        GB = pc.get([128, 16, 16], F32, "GB", m.new_dsem())
        GM = pc.get([128, 16, 16], F32, "GM")
        TOP = pc.get([128, 16, 8], F32, "TOP")
        THR = pc.get([128, 16], F32, "THR")
        SEL = pc.get([128, 16, 16], F32, "SEL")
        PT = [pc.get([128, 512], BF16, f"PT{i}") for i in range(2)]
        ACC = [pc.get([128, 2, 129], F32, f"ACC{i}") for i in range(2)]
        RC = [pc.get([128, 2], F32, f"RC{i}") for i in range(2)]
        YB = [pc.get([128, 2, 128], BF16, f"YB{i}") for i in range(2)]
        assert pc.off <= AR.nbytes, pc.off
        ps = new_banks()
        psS = [ps[4], ps[5]]
        psO = [ps[6], ps[7]]

        m.dma(sp, GB[:], gbias_d, writes=[GB])
        m.op(dve, lambda e: e.memset(VA[:, :, 128:129], 1.0), writes=[VA])

        load_w(wsl[0], w_mb_d, 0, 3)
        pidx = 0
        ridx = 0
        for hd in range(NH):
            wt = wsl[hd % 2]
            if hd + 1 < NH:
                load_w(wsl[(hd + 1) % 2], w_mb_d, hd + 1, 3)
            for tg in range(8):
                own = tg >= 4
                xt = xT_own if own else xT_pre
                col0 = (tg % 4) * 512
                cs = CS[tg % 2]
                csd = cs_own_d if own else cs_pre_d
                m.dma(sp, cs[:], csd[:, :, col0:col0 + 512], writes=[cs])
                proj_fm(ps[0], wt, 1, xt, col0, 512)
                proj_tm(ps[3], wt, 2, xt, col0)
                if own:
                    proj_fm(ps[1], wt, 0, xt, col0, 512)
                for (bank, dst, dcol) in ([(ps[0], KTh, tg * 512)] + ([(ps[1], QTh, (tg - 4) * 512)] if own else [])):
                    ra, rb = RA[ridx % 2], RB[ridx % 2]
                    ridx += 1
                    m.op(dve, lambda e, bank=bank, ra=ra: e.tensor_tensor(ra[:], bank[:], cs[:, 0, :], ALU.mult), reads=[bank, cs], writes=[ra])
                    m.op(dve, lambda e, bank=bank, rb=rb: e.tensor_tensor(rb[0:64, :], bank[64:128, :], cs[0:64, 1, :], ALU.mult),
                         reads=[bank, cs], writes=[rb])
                    m.op(dve, lambda e, bank=bank, rb=rb: e.tensor_tensor(rb[64:128, :], bank[0:64, :], cs[64:128, 1, :], ALU.mult),
                         reads=[bank, cs], writes=[rb])
                    m.op(pool, lambda e, ra=ra, rb=rb, dst=dst, dcol=dcol: e.tensor_tensor(dst[:, dcol:dcol + 512], ra[:], rb[:], ALU.add),
                         reads=[ra, rb], writes=[dst])
                m.op(act, lambda e: e.activation(VA[:, tg * 4:(tg + 1) * 4, 0:128], ps[3][:].rearrange("p (a b) -> p a b", b=128), AF.Copy),
                     reads=[ps[3]], writes=[VA])
            m.op(dve, lambda e: e.tensor_reduce(KM[:], KTh[:].rearrange("p (n k) -> p n k", k=256), AX.X, ALU.add), reads=[KTh], writes=[KM])
            m.op(dve, lambda e: e.tensor_scalar(KMB[:], KM[:], 1.0 / 256.0, None, ALU.mult), reads=[KM], writes=[KMB])
            for tt in range(16):
                m.op(pe, lambda e, tt=tt: e.matmul(ps[2][:, tt * 16:(tt + 1) * 16], lhsT=QTh[:, tt * 128:(tt + 1) * 128], rhs=KMB[:],
                                                   start=True, stop=True), reads=[QTh, KMB], writes=[ps[2]], inc=(tt == 15))
            m.op(dve, lambda e: e.tensor_tensor(GM[:].rearrange("p a b -> p (a b)"), ps[2][:, 0:256], GB[:].rearrange("p a b -> p (a b)"), ALU.add),
                 reads=[ps[2], GB], writes=[GM])
            for tt in range(16):
                m.op(dve, lambda e, tt=tt: e.max(TOP[:, tt, :], GM[:, tt, :]), reads=[GM], writes=[TOP])
            m.op(dve, lambda e: e.tensor_scalar(THR[:], TOP[:, :, 2], -1.0e29, None, ALU.max), reads=[TOP], writes=[THR])
            m.op(dve, lambda e: e.tensor_tensor(SEL[:], GM[:], THR[:].rearrange("p (a o) -> p a o", o=1).to_broadcast([128, 16, 16]), ALU.is_ge),
                 reads=[GM, THR], writes=[SEL])
            for j in range(8):
                acc, rc, yb = ACC[j % 2], RC[j % 2], YB[j % 2]
                q0 = j * 256
                nown = 8 + j
                sS, sO, pt = psS[pidx % 2], psO[pidx % 2], PT[pidx % 2]
                pidx += 1
                k0 = nown * 256
                m.op(pe, lambda e: e.matmul(sS[:, 0:256], lhsT=KTh[:, k0:k0 + 128], rhs=QTh[:, q0:q0 + 256], start=True, stop=True),
                     reads=[KTh, QTh], writes=[sS], inc=False)
                m.op(pe, lambda e: e.matmul(sS[:, 384:512], lhsT=KTh[:, k0 + 128:k0 + 256], rhs=QTh[:, q0 + 128:q0 + 256], start=True, stop=True),
                     reads=[KTh, QTh], writes=[sS])
                m.op(act, lambda e: e.activation(pt[:, 0:256], sS[:, 0:256], AF.Exp, scale=SCALE), reads=[sS], writes=[pt])
                m.op(act, lambda e: e.activation(pt[:, 384:512], sS[:, 384:512], AF.Exp, scale=SCALE), reads=[sS], writes=[pt])
                m.op(pool, lambda e: e.tensor_tensor(pt[:, 0:128], pt[:, 0:128], tri_b[:], ALU.mult), reads=[pt, tri_b], writes=[pt])
                m.op(pool, lambda e: e.tensor_tensor(pt[:, 384:512], pt[:, 384:512], tri_b[:], ALU.mult), reads=[pt, tri_b], writes=[pt])
                sO3 = sO[:, 0:258].rearrange("p (a b) -> p a b", b=129)
                m.op(pe, lambda e: e.matmul(sO3[:, 0, :], lhsT=pt[:, 0:128], rhs=VA[:, 2 * nown, :], start=True, stop=True),
                     reads=[pt, VA], writes=[sO], inc=False)
                m.op(pe, lambda e: e.matmul(sO3[:, 1, :], lhsT=pt[:, 128:256], rhs=VA[:, 2 * nown, :], start=True, stop=False),
                     reads=[pt, VA], writes=[sO], inc=False)
                m.op(pe, lambda e: e.matmul(sO3[:, 1, :], lhsT=pt[:, 384:512], rhs=VA[:, 2 * nown + 1, :], start=False, stop=True),
                     reads=[pt, VA], writes=[sO])
                m.op(act, lambda e: e.activation(acc[:].rearrange("p a b -> p (a b)"), sO[:, 0:258], AF.Copy), reads=[sO], writes=[acc])
                for n in range(nown):
                    sS, sO, pt = psS[pidx % 2], psO[pidx % 2], PT[pidx % 2]
                    pidx += 1
                    k0 = n * 256
                    for kt in range(2):
                        m.op(pe, lambda e, kt=kt: e.matmul(sS[:, kt * 256:(kt + 1) * 256], lhsT=KTh[:, k0 + kt * 128:k0 + (kt + 1) * 128],
                                                           rhs=QTh[:, q0:q0 + 256], start=True, stop=True),
                             reads=[KTh, QTh], writes=[sS], inc=(kt == 1))
                    m.op(act, lambda e: e.activation(pt[:], sS[:], AF.Exp, scale=SCALE), reads=[sS], writes=[pt])
                    sO3 = sO[:, 0:258].rearrange("p (a b) -> p a b", b=129)
                    for qs in range(2):
                        for kt in range(2):
                            m.op(pe, lambda e, qs=qs, kt=kt: e.matmul(sO3[:, qs, :], lhsT=pt[:, kt * 256 + qs * 128: kt * 256 + (qs + 1) * 128],
                                                                      rhs=VA[:, 2 * n + kt, :], start=(kt == 0), stop=(kt == 1)),
                                 reads=[pt, VA], writes=[sO], inc=(qs == 1 and kt == 1))
                    for qs in range(2):
                        m.op(dve, lambda e, qs=qs: e.scalar_tensor_tensor(acc[:, qs, :], sO3[:, qs, :], SEL[:, 2 * j + qs, n:n + 1], acc[:, qs, :],
                                                                          ALU.mult, ALU.add), reads=[sO, SEL, acc], writes=[acc])
                m.op(dve, lambda e: e.reciprocal(rc[:], acc[:, :, 128]), reads=[acc], writes=[rc])
                for qs in range(2):
                    m.op(dve, lambda e, qs=qs: e.tensor_scalar(yb[:, qs, :], acc[:, qs, 0:128], rc[:, qs:qs + 1], None, ALU.mult),
                         reads=[acc, rc], writes=[yb])
                p2b = ps[2][:].bitcast(BF16)
                for qs in range(2):
                    m.op(pe, lambda e, qs=qs: e.transpose(p2b[:, qs * 128:(qs + 1) * 128], yb[:, qs, :], ident_b[:]),
                         reads=[yb, ident_b], writes=[ps[2]], inc=(qs == 1))
                m.op(act, lambda e: e.activation(ybT[:, hd, q0:q0 + 256], p2b[:, 0:256], AF.Copy), reads=[ps[2]], writes=[ybT])
        m.barrier()
        if debug:
            m.dma(sp, dbg["yb"], ybT[:], reads=[ybT])
        if stop_after == "C":
            return _finish(nc, m, [yaT, ybT], out_d)

        mixT = AR.at(1 * SLOT, [128, 8, T], BF16, "mixT")
        pd = Bump(AR, OFF_PHASE)
        w_o = pd.get([128, 8, D], BF16, "w_o", m.new_dsem())
        off_d2 = pd.off
        wsl = [pd.get([128, 4 * 8 * 128], BF16, f"wD{i}", m.new_dsem()) for i in range(2)]
        for b_ in wsl:
            b_.t = b_.t.rearrange("p (j k c) -> p j k c", j=4, k=8)
        EA = [pd.get([128, 512], F32, f"EA{i}") for i in range(2)]
        EB = [pd.get([128, 512], F32, f"EB{i}") for i in range(2)]
        M1 = [pd.get([128, 512], F32, f"M1{i}") for i in range(2)]
        M2 = [pd.get([128, 512], F32, f"M2{i}") for i in range(2)]
        assert pd.off <= AR.nbytes, pd.off
        ps = new_banks()

        def load_wd(wslot, cc):
            load_w(wslot, w_d1_d, cc, 4)

        load_wd(wsl[0], 0)
        it = 0
        for cc in range(8):
            wt = wsl[cc % 2]
            if cc + 1 < 8:
                load_wd(wsl[(cc + 1) % 2], cc + 1)
            else:
                for kc in range(8):
                    m.dma(pool, w_o[:, kc, :], w_o_d[:, kc, :], writes=[w_o])
            for tg in range(4):
                col0 = tg * 512
                b0 = (it % 2) * 4
                ea, eb, m1, m2 = EA[it % 2], EB[it % 2], M1[it % 2], M2[it % 2]
                it += 1
                proj_fm(ps[b0 + 0], wt, 0, yaT, col0, 512)
                proj_fm(ps[b0 + 1], wt, 1, ybT, col0, 512)
                proj_fm(ps[b0 + 2], wt, 2, xT_own, col0, 512)
                proj_fm(ps[b0 + 3], wt, 3, xT_own, col0, 512)
                m.op(act, lambda e: e.activation(ea[:], ps[b0 + 2][:], AF.Exp, scale=-1.0, bias=bgn[:, cc:cc + 1]), reads=[ps[b0 + 2], bgn], writes=[ea])
                m.op(act, lambda e: e.activation(eb[:], ps[b0 + 3][:], AF.Exp, scale=-1.0, bias=bgn[:, 8 + cc:9 + cc]), reads=[ps[b0 + 3], bgn], writes=[eb])
                m.op(pool, lambda e: e.tensor_scalar(ea[:], ea[:], 1.0, None, ALU.add), reads=[ea], writes=[ea])
                m.op(pool, lambda e: e.tensor_scalar(eb[:], eb[:], 1.0, None, ALU.add), reads=[eb], writes=[eb])
                m.op(dve, lambda e: e.reciprocal(ea[:], ea[:]), reads=[ea], writes=[ea])
                m.op(dve, lambda e: e.reciprocal(eb[:], eb[:]), reads=[eb], writes=[eb])
                m.op(dve, lambda e: e.tensor_tensor(m1[:], ps[b0 + 0][:], ea[:], ALU.mult), reads=[ps[b0 + 0], ea], writes=[m1])
                m.op(dve, lambda e: e.tensor_tensor(m2[:], ps[b0 + 1][:], eb[:], ALU.mult), reads=[ps[b0 + 1], eb], writes=[m2])
                m.op(pool, lambda e: e.tensor_tensor(mixT[:, cc, col0:col0 + 512], m1[:], m2[:], ALU.add), reads=[m1, m2], writes=[mixT])
        m.barrier()

        x1T = AR.at(3 * SLOT, [128, 8, T], BF16, "x1T")
        pd2 = Bump(AR, off_d2)
        lnp = pd2.get([128, 4, D], F32, "lnp", m.new_dsem())
        XR = [pd2.get([128, D], F32, f"XR{i}", m.new_dsem()) for i in range(2)]
        RR = [pd2.get([128, D], F32, f"RR{i}", m.new_dsem()) for i in range(2)]
        X1B = [pd2.get([128, D], BF16, f"X1B{i}") for i in range(2)]
        STt = [pd2.get([128, 12], F32, f"ST{i}") for i in range(2)]
        MV = [pd2.get([128, 2], F32, f"MV{i}") for i in range(2)]
        RSD = [pd2.get([128, 2], F32, f"RSD{i}") for i in range(2)]
        assert pd2.off <= AR.nbytes, pd2.off
        ps = new_banks()
        m.dma(sp, lnp[:], lnp_d, writes=[lnp])

        def layer_norm(src_banks, xr, rr, st_, mv, rsd, gi):
            for hf in range(2):
                sl = slice(hf * 512, (hf + 1) * 512)
                m.op(dve, lambda e, hf=hf, sl=sl: e.scalar_tensor_tensor(rr[:, sl], xr[:, sl], ALPHA, src_banks[hf][:], ALU.mult, ALU.add),
                     reads=[xr, src_banks[hf]], writes=[rr])
            for hf in range(2):
                m.op(dve, lambda e, hf=hf: e.bn_stats(st_[:, hf * 6:(hf + 1) * 6], rr[:, hf * 512:(hf + 1) * 512]), reads=[rr], writes=[st_])
            m.op(dve, lambda e: e.bn_aggr(mv[:], st_[:]), reads=[st_], writes=[mv])
            m.op(act, lambda e: e.activation(rsd[:, 0:1], mv[:, 1:2], AF.Ln, bias=LN_EPS), reads=[mv], writes=[rsd])
            m.op(act, lambda e: e.activation(rsd[:, 0:1], rsd[:, 0:1], AF.Exp, scale=-0.5), reads=[rsd], writes=[rsd])
            m.op(dve, lambda e: e.scalar_tensor_tensor(rsd[:, 1:2], mv[:, 0:1], -1.0, rsd[:, 0:1], ALU.mult, ALU.mult), reads=[mv, rsd], writes=[rsd])
            m.op(act, lambda e: e.activation(rr[:], rr[:], AF.Identity, scale=rsd[:, 0:1], bias=rsd[:, 1:2]), reads=[rr, rsd], writes=[rr])
            m.op(dve, lambda e: e.tensor_tensor(rr[:], rr[:], lnp[:, gi, :], ALU.mult), reads=[rr, lnp], writes=[rr])
            m.op(pool, lambda e: e.tensor_tensor(rr[:], rr[:], lnp[:, gi + 1, :], ALU.add), reads=[rr, lnp], writes=[rr])

        for tt in range(16):
            xr, rr, x1b, st_, mv, rsd = XR[tt % 2], RR[tt % 2], X1B[tt % 2], STt[tt % 2], MV[tt % 2], RSD[tt % 2]
            b0 = (tt % 2) * 4
            tsl = slice(tt * 128, (tt + 1) * 128)
            m.dma(sp, xr[:], x_tok_d[tsl, :], writes=[xr])
            for hf in range(2):
                for cc in range(8):
                    m.op(pe, lambda e, hf=hf, cc=cc: e.matmul(ps[b0 + hf][:], lhsT=mixT[:, cc, tsl], rhs=w_o[:, cc, hf * 512:(hf + 1) * 512],
                                                              start=(cc == 0), stop=(cc == 7)),
                         reads=[mixT, w_o], writes=[ps[b0 + hf]], inc=(cc == 7))
            layer_norm([ps[b0], ps[b0 + 1]], xr, rr, st_, mv, rsd, 0)
            m.dma(sp, x1_d[tsl, :], rr[:], reads=[rr])
            if debug:
                m.dma(sp, dbg["x1"][tsl, :], rr[:], reads=[rr])
            m.op(act, lambda e: e.activation(x1b[:], rr[:], AF.Copy), reads=[rr], writes=[x1b])
            pb_ = ps[b0 + 2][:].bitcast(BF16)
            for kc in range(8):
                m.op(pe, lambda e, kc=kc: e.transpose(pb_[:, kc * 128:(kc + 1) * 128], x1b[:, kc * 128:(kc + 1) * 128], ident_b[:]),
                     reads=[x1b, ident_b], writes=[ps[b0 + 2]], inc=(kc == 7))
            m.op(dve, lambda e: e.tensor_copy(x1T[:, :, tsl], pb_.rearrange("p (k t) -> p k t", t=128)), reads=[ps[b0 + 2]], writes=[x1T])
        m.barrier()
        if stop_after == "D":
            return _finish(nc, m, [], out_d)

        aT = AR.at(0, [128, NFC, 1024], BF16, "aT")
        w_fd = AR.at(45056, [128, NFC, D], BF16, "w_fd", m.new_dsem())
        assert 45056 + NFC * D * 2 <= 3 * SLOT
        pe_ = Bump(AR, OFF_PHASE)
        wsl = [pe_.get([128, 4 * 8 * 128], BF16, f"wE{i}", m.new_dsem()) for i in range(2)]
        for b_ in wsl:
            b_.t = b_.t.rearrange("p (j k c) -> p j k c", j=4, k=8)
        EG = [pe_.get([128, 512], F32, f"EG{i}") for i in range(2)]
        SGf = [pe_.get([128, 512], F32, f"SGf{i}") for i in range(2)]
        lnp = pe_.get([128, 4, D], F32, "lnp2", m.new_dsem())
        XR = [pe_.get([128, D], F32, f"XRe{i}", m.new_dsem()) for i in range(2)]
        RR = [pe_.get([128, D], F32, f"RRe{i}", m.new_dsem()) for i in range(2)]
        STt = [pe_.get([128, 12], F32, f"STe{i}") for i in range(2)]
        MV = [pe_.get([128, 2], F32, f"MVe{i}") for i in range(2)]
        RSD = [pe_.get([128, 2], F32, f"RSDe{i}") for i in range(2)]
        assert pe_.off <= AR.nbytes, pe_.off
        ps = new_banks()
        m.dma(sp, lnp[:], lnp_d, writes=[lnp])
        for fc in range(NFC):
            m.dma(pool, w_fd[:, fc, :], w_fd_d[:, fc, :], writes=[w_fd])

        def load_we(wslot, fc):
            load_w(wslot, w_e1_d, fc, 2)

        it = 0
        wi = 0
        for half in range(2):
            load_we(wsl[wi % 2], 0)
            for fc in range(NFC):
                wt = wsl[wi % 2]
                wi += 1
                if fc + 1 < NFC:
                    load_we(wsl[wi % 2], fc + 1)
                for tg in range(2):
                    col0 = half * 1024 + tg * 512
                    b0 = (it % 2) * 2
                    eg, sg = EG[it % 2], SGf[it % 2]
                    it += 1
                    proj_fm(ps[b0], wt, 0, x1T, col0, 512)
                    proj_fm(ps[b0 + 1], wt, 1, x1T, col0, 512)
                    m.op(act, lambda e: e.activation(eg[:], ps[b0][:], AF.Exp, scale=-1.0), reads=[ps[b0]], writes=[eg])
                    m.op(pool, lambda e: e.tensor_scalar(eg[:], eg[:], 1.0, None, ALU.add), reads=[eg], writes=[eg])
                    m.op(dve, lambda e: e.reciprocal(eg[:], eg[:]), reads=[eg], writes=[eg])
                    m.op(dve, lambda e: e.tensor_tensor(sg[:], ps[b0][:], eg[:], ALU.mult), reads=[ps[b0], eg], writes=[sg])
                    m.op(dve, lambda e: e.tensor_tensor(aT[:, fc, tg * 512:(tg + 1) * 512], ps[b0 + 1][:], sg[:], ALU.mult),
                         reads=[ps[b0 + 1], sg], writes=[aT])
            for t8 in range(8):
                tt = half * 8 + t8
                xr, rr, st_, mv, rsd = XR[tt % 2], RR[tt % 2], STt[tt % 2], MV[tt % 2], RSD[tt % 2]
                b0 = 4 + (tt % 2) * 2
                tsl = slice(tt * 128, (tt + 1) * 128)
                lsl = slice(t8 * 128, (t8 + 1) * 128)
                m.dma(sp, xr[:], x1_d[tsl, :], writes=[xr])
                for hf in range(2):
                    for fc in range(NFC):
                        m.op(pe, lambda e, hf=hf, fc=fc: e.matmul(ps[b0 + hf][:], lhsT=aT[:, fc, lsl], rhs=w_fd[:, fc, hf * 512:(hf + 1) * 512],
                                                                  start=(fc == 0), stop=(fc == NFC - 1)),
                             reads=[aT, w_fd], writes=[ps[b0 + hf]], inc=(fc == NFC - 1))
                layer_norm([ps[b0], ps[b0 + 1]], xr, rr, st_, mv, rsd, 2)
                m.dma(sp, out_d[tsl, :], rr[:], reads=[rr])
        m.barrier()
        return _finish(nc, m, [], out_d)


def _finish(nc, m, bufs, out_d):
    m.barrier()
    return nc


def _r_w(w):
    K, N = w.shape
    return np.ascontiguousarray(w.reshape(K // 128, 128, N).transpose(1, 0, 2))


def _rope_tables(pos):
    half = 64
    inv_freq = (10000.0 ** (-np.arange(half, dtype=np.float32) / np.float32(half))).astype(np.float32)
    ang = (pos.astype(np.float32)[:, None] * inv_freq[None, :]).astype(np.float32)
    cos = np.cos(ang).astype(np.float32).T
    sin = np.sin(ang).astype(np.float32).T
    cs = np.empty((128, 2, pos.shape[0]), np.float32)
    cs[0:64, 0] = cos
    cs[64:128, 0] = cos
    cs[0:64, 1] = -sin
    cs[64:128, 1] = sin
    return cs


def make_in_maps(x, w_in, lb_logits, hgrn_norm_w, w_branch_a, w_branch_b, b_gate, w_out,
                 ln1_w, ln1_b, w_ffn_in, w_ffn_down, ln2_w, ln2_b):
    x = np.asarray(x, np.float32)
    f = lambda a: np.asarray(a, np.float32)
    win = _r_w(f(w_in)[0])
    wa, wb = _r_w(f(w_branch_a)[0]), _r_w(f(w_branch_b)[0])
    wfi = _r_w(f(w_ffn_in)[0])

    def tiles(w, col0, n):
        return w[:, :, col0:col0 + n * 128].reshape(128, 8, n, 128).transpose(0, 2, 1, 3)

    w_hg = np.stack([tiles(win, j * 1024, 8) for j in range(4)], axis=2)
    w_mb = np.stack([tiles(win, 4096 + j * 1024, 8) for j in range(3)], axis=2)
    w_d1 = np.stack([tiles(wa, 0, 8), tiles(wb, 0, 8), tiles(win, 7168, 8), tiles(win, 8192, 8)], axis=2)
    w_e1 = np.stack([tiles(wfi, 0, NFC), tiles(wfi, DFF, NFC)], axis=2)
    shared = {
        "w_hg": np.ascontiguousarray(w_hg).reshape(128, NH, 4 * 1024),
        "w_mb": np.ascontiguousarray(w_mb).reshape(128, NH, 3 * 1024),
        "w_d1": np.ascontiguousarray(w_d1).reshape(128, 8, 4 * 1024),
        "w_e1": np.ascontiguousarray(w_e1).reshape(128, NFC, 2 * 1024),
        "w_o_r": _r_w(f(w_out)[0]),
        "w_fd_r": _r_w(f(w_ffn_down)[0]),
        "lbl": np.ascontiguousarray(f(lb_logits).reshape(2, 8, 128).transpose(2, 0, 1)),
        "nw": np.ascontiguousarray(f(hgrn_norm_w)[0].reshape(8, 128).T),
        "bg": np.ascontiguousarray(f(b_gate)[0].reshape(16, 128).T),
        "lnp": np.ascontiguousarray(np.broadcast_to(
            np.stack([f(ln1_w)[0], f(ln1_b)[0], f(ln2_w)[0], f(ln2_b)[0]])[None], (128, 4, D))),
        "tri": np.triu(np.ones((128, 128), np.float32)),
        "ident": np.eye(128, dtype=np.float32),
    }
    sm = np.ones((128, 512), np.float32)
    sm[:, 0::128] = 0.0
    shared["scanm"] = sm
    in_maps = []
    for c in range(8):
        b, h = c // 2, c % 2
        xo = x[b, h * T:(h + 1) * T, :]
        xp = x[b, 0:T, :] if h == 1 else np.zeros((T, D), np.float32)
        mp = dict(shared)
        mp["x_tok"] = np.ascontiguousarray(xo)
        mp["xT_own"] = np.ascontiguousarray(xo.T.reshape(8, 128, T).transpose(1, 0, 2))
        mp["xT_pre"] = np.ascontiguousarray(xp.T.reshape(8, 128, T).transpose(1, 0, 2))
        mp["cs_own"] = _rope_tables(np.arange(h * T, (h + 1) * T))
        mp["cs_pre"] = _rope_tables(np.arange(0, T))
        gb = np.full((16, 16), NEG, np.float32)
        for tt in range(16):
            j = tt // 2
            for n in range(16):
                if n < 8 + j and (h == 1 or n >= 8):
                    gb[tt, n] = 0.0
        mp["gbias"] = np.ascontiguousarray(np.broadcast_to(gb[None], (128, 16, 16)))
        in_maps.append(mp)
    return in_maps


_NC_CACHE = {}


def kernel(**inputs):
    in_maps = make_in_maps(**inputs)
    if "nc" not in _NC_CACHE:
        _NC_CACHE["nc"] = build_program()
    nc = _NC_CACHE["nc"]
    res = run_bass_kernel_spmd(nc, in_maps, core_ids=list(range(8)))
    out = np.empty((4, 2 * T, D), np.float32)
    for c in range(8):
        b, h = c // 2, c % 2
        out[b, h * T:(h + 1) * T, :] = np.asarray(res.results[c]["out"], np.float32)
    return out
```
